# Optimizing a Trainium2 kernel written in Bass

```python
import jax, jax.numpy as jnp
from jax import lax
import numpy as np

D_MODEL = 2048
BATCH = 16
SEQ = 2048
DEPTH = 2

CHUNK = 64
CONV_WIDTH = 1024
CONV_K = 3
HG_HEADS = 8
HG_DK = 128
HG_DV = 128
HG_WIDTH = HG_HEADS * HG_DK
HG_CHUNK = 16
ATT_HEADS = 8
ATT_DH = 128
ATT_WIDTH = ATT_HEADS * ATT_DH
ATT_LEFT_CHUNKS = 8
BAND = ATT_LEFT_CHUNKS + 1
MAX_REL_DIST = 256
D_FF = 4 * D_MODEL
PLE_DIM = 256
N_BRANCH = 3
EPS = 1e-6
LB_FLOOR = 1e-30
MASK_VALUE = -1e30
IN_COLS = 3 * CONV_WIDTH + 4 * HG_WIDTH + 3 * ATT_WIDTH + N_BRANCH * D_MODEL

kernel_name = "hybrid_gated_conv_hgrn2_chunkattn_encoder"


def rms_norm(x, g):
    x32 = x.astype(jnp.float32)
    y = x32 * lax.rsqrt(jnp.mean(x32 * x32, axis=-1, keepdims=True) + EPS)
    return (y * g.astype(jnp.float32)).astype(x.dtype)


def causal_depthwise_conv(u, w):
    return lax.conv_general_dilated(
        u, w[:, None, :].astype(u.dtype), window_strides=(1,),
        padding=[(CONV_K - 1, 0)], dimension_numbers=("NWC", "WIO", "NWC"),
        feature_group_count=u.shape[-1])


def hgrn2(q, f_logit, i, g, lb, norm_g):
    B, S, _ = q.shape
    n = S // HG_CHUNK
    f32 = jnp.float32
    shp = (B, n, HG_CHUNK, HG_HEADS, HG_DK)
    z = f_logit.astype(f32)
    lb = lb.astype(f32)
    log_f = jnp.logaddexp(jnp.log(jnp.maximum(lb, LB_FLOOR)),
                          jnp.log1p(-lb) + jax.nn.log_sigmoid(z))
    k = (1.0 - lb) * jax.nn.sigmoid(-z)
    qc = q.astype(f32).reshape(shp)
    kc = k.reshape(shp)
    vc = i.astype(f32).reshape(B, n, HG_CHUNK, HG_HEADS, HG_DV)
    G = jnp.cumsum(log_f.reshape(shp), axis=2)
    causal = jnp.tril(jnp.ones((HG_CHUNK, HG_CHUNK), bool))[None, None, :, :, None, None]
    diff = G[:, :, :, None] - G[:, :, None, :]
    decay = jnp.where(causal, jnp.exp(jnp.where(causal, diff, 0.0)), 0.0)
    A = jnp.einsum("bnthk,bnshk,bntshk->bnhts", qc, kc, decay)
    o_intra = jnp.einsum("bnhts,bnshv->bnthv", A, vc)
    G_last = G[:, :, -1]
    q_dec = qc * jnp.exp(G)
    k_dec = kc * jnp.exp(G_last[:, :, None] - G)
    chunk_decay = jnp.exp(G_last)

    def step(state, xs):
        qd, kd, v, dec = xs
        o = jnp.einsum("bthk,bhkv->bthv", qd, state)
        state = dec[..., None] * state + jnp.einsum("bshk,bshv->bhkv", kd, v)
        return state, o

    xs = (jnp.moveaxis(q_dec, 1, 0), jnp.moveaxis(k_dec, 1, 0),
          jnp.moveaxis(vc, 1, 0), jnp.moveaxis(chunk_decay, 1, 0))
    s0 = jnp.zeros((B, HG_HEADS, HG_DK, HG_DV), f32)
    _, o_inter = lax.scan(step, s0, xs)
    o = (o_intra + jnp.moveaxis(o_inter, 0, 1)).reshape(B, S, HG_HEADS, HG_DV)
    o = o * lax.rsqrt(jnp.mean(o * o, axis=-1, keepdims=True) + EPS) * norm_g.astype(f32)
    o = o.reshape(B, S, HG_WIDTH) * jax.nn.silu(g.astype(f32))
    return o.astype(q.dtype)


def rel_bias_index():
    a = np.arange(CHUNK)[:, None, None]
    j = np.arange(BAND)[None, :, None]
    b = np.arange(CHUNK)[None, None, :]
    rel = (BAND - 1 - j) * CHUNK + a - b
    idx = np.clip(rel, -MAX_REL_DIST, MAX_REL_DIST) + MAX_REL_DIST
    return idx.reshape(CHUNK, BAND * CHUNK)


def chunk_band_attention(q, k, v, rel_table):
    B, S, _ = q.shape
    n = S // CHUNK
    f32 = jnp.float32
    qc = q.reshape(B, n, CHUNK, ATT_HEADS, ATT_DH)
    pad = ((0, 0), (ATT_LEFT_CHUNKS, 0), (0, 0), (0, 0), (0, 0))
    kp = jnp.pad(k.reshape(B, n, CHUNK, ATT_HEADS, ATT_DH), pad)
    vp = jnp.pad(v.reshape(B, n, CHUNK, ATT_HEADS, ATT_DH), pad)
    band_idx = jnp.arange(n)[:, None] + jnp.arange(BAND)[None, :]
    kb = kp[:, band_idx].reshape(B, n, BAND * CHUNK, ATT_HEADS, ATT_DH)
    vb = vp[:, band_idx].reshape(B, n, BAND * CHUNK, ATT_HEADS, ATT_DH)
    valid = jnp.repeat(band_idx >= ATT_LEFT_CHUNKS, CHUNK, axis=1)
    bias = rel_table[:, rel_bias_index()].astype(f32)
    s = jnp.einsum("bnqhd,bnkhd->bnhqk", qc, kb).astype(f32) * (ATT_DH ** -0.5) + bias[None, None]
    s = jnp.where(valid[None, :, None, None, :], s, MASK_VALUE)
    pr = jax.nn.softmax(s, axis=-1).astype(v.dtype)
    o = jnp.einsum("bnhqk,bnkhd->bnqhd", pr, vb)
    return o.reshape(B, S, ATT_WIDTH)


def setup_inputs(seed: int = 0) -> dict:
    key = jax.random.key(seed)
    ks = jax.random.split(key, 20)
    f32 = jnp.float32

    def nrm(k, shape, scale):
        return jax.random.normal(k, shape, f32) * scale

    return {
        "x": nrm(ks[0], (BATCH, SEQ, D_MODEL), 1.0),
        "p": nrm(ks[1], (DEPTH, BATCH, SEQ, PLE_DIM), 1.0),
        "w_in": nrm(ks[2], (DEPTH, D_MODEL, IN_COLS), D_MODEL ** -0.5),
        "conv_w": nrm(ks[3], (DEPTH, CONV_K, CONV_WIDTH), CONV_K ** -0.5),
        "hg_lb_logits": nrm(ks[4], (DEPTH, HG_WIDTH), 0.5),
        "hg_norm_g": 1.0 + nrm(ks[5], (DEPTH, HG_DV), 0.02),
        "att_rel_bias": nrm(ks[6], (DEPTH, ATT_HEADS, 2 * MAX_REL_DIST + 1), 0.2),
        "w_branch": nrm(ks[7], (DEPTH, N_BRANCH, CONV_WIDTH, D_MODEL), CONV_WIDTH ** -0.5),
        "w_o": nrm(ks[8], (DEPTH, D_MODEL, D_MODEL), D_MODEL ** -0.5),
        "w_ff1": nrm(ks[9], (DEPTH, D_MODEL, D_FF), D_MODEL ** -0.5),
        "w_ff2": nrm(ks[10], (DEPTH, D_FF, D_MODEL), D_FF ** -0.5),
        "w_ple_in": nrm(ks[11], (DEPTH, PLE_DIM, D_MODEL), PLE_DIM ** -0.5),
        "w_ple_gate": nrm(ks[12], (DEPTH, D_MODEL, D_MODEL), D_MODEL ** -0.5),
        "g_mix": 1.0 + nrm(ks[13], (DEPTH, D_MODEL), 0.02),
        "g_ff": 1.0 + nrm(ks[14], (DEPTH, D_MODEL), 0.02),
        "g_ple": 1.0 + nrm(ks[15], (DEPTH, D_MODEL), 0.02),
        "g_final": 1.0 + nrm(ks[16], (D_MODEL,), 0.02),
    }


def reference(x, p, w_in, conv_w, hg_lb_logits, hg_norm_g, att_rel_bias, w_branch, w_o,
              w_ff1, w_ff2, w_ple_in, w_ple_gate, g_mix, g_ff, g_ple, g_final):
    lb_sm = jax.nn.softmax(hg_lb_logits.astype(jnp.float32), axis=0)
    lb_all = jnp.cumsum(lb_sm, axis=0) - lb_sm[0]
    split_points = [int(v) for v in np.cumsum(
        [CONV_WIDTH] * 3 + [HG_WIDTH] * 4 + [ATT_WIDTH] * 3 + [D_MODEL] * 2)]
    h = x
    for l in range(DEPTH):
        xn = rms_norm(h, g_mix[l])
        proj = xn @ w_in[l]
        (cb, cc, ch, hq, hf, hi, hg, aq, ak, av,
         gate_a, gate_b, gate_c) = jnp.split(proj, split_points, axis=-1)
        y_conv = cb * causal_depthwise_conv(cc * ch, conv_w[l])
        y_hg = hgrn2(hq, hf, hi, hg, lb_all[l], hg_norm_g[l])
        y_att = chunk_band_attention(aq, ak, av, att_rel_bias[l])
        merged = (jax.nn.sigmoid(gate_a) * (y_conv @ w_branch[l, 0])
                  + jax.nn.sigmoid(gate_b) * (y_hg @ w_branch[l, 1])
                  + jax.nn.sigmoid(gate_c) * (y_att @ w_branch[l, 2]))
        h = h + merged @ w_o[l]
        hn = rms_norm(h, g_ff[l])
        h = h + jnp.square(jax.nn.relu(hn @ w_ff1[l])) @ w_ff2[l]
        ple_gate = jax.nn.sigmoid(rms_norm(h, g_ple[l]) @ w_ple_gate[l])
        h = h + ple_gate * (p[l] @ w_ple_in[l])
    return rms_norm(h, g_final)
```

```python
import numpy as np
from contextlib import ExitStack
import concourse.bass as bass
import concourse.mybir as mybir
from concourse.bass_utils import run_bass_kernel_spmd

F32 = mybir.dt.float32
BF16 = mybir.dt.bfloat16
AF = mybir.ActivationFunctionType
ALU = mybir.AluOpType

D = 2048
T = 512
NCH = 16
L = 2
EPS = 1e-6
LW = 4096
NSLOT = 4
NL = 160
PIECE = 8
NCAST = 8
COL = dict(cb=0, cc=1024, ch=2048, hq=3072, hf=4096, hi=5120, hg=6144, aq=7168, ak=8192,
           av=9216, ga=10240, gb=12288, gc=14336)
NEG = -30000.0
PRM_L = 81
NPRM = 2 * PRM_L + 16
NCST = 128 + 128 + 128 + 512


class Buf:
    __slots__ = ("name", "w", "r")

    def __init__(self, name=""):
        self.name = name
        self.w = None
        self.r = {}


class Prog:
    ENG = ("pe", "act", "dve", "pool", "sp")

    def __init__(self):
        self.streams = {e: [] for e in self.ENG}
        self.cnt = {}
        self.waited = {e: {} for e in self.ENG}
        self.last_dma = {}

    def _need(self, eng, tok, waits):
        if tok is None:
            return
        s, v = tok
        if eng == "pe" and s == "pe":
            return
        if self.waited[eng].get(s, 0) >= v:
            return
        if waits.get(s, 0) < v:
            waits[s] = v

    def op(self, eng, fn, reads=(), writes=(), sem=None, signal=True):
        waits = {}
        for b in reads:
            self._need(eng, b.w, waits)
        for b in writes:
            self._need(eng, b.w, waits)
            for s, v in b.r.items():
                self._need(eng, (s, v), waits)
        if sem is not None:
            self._need(eng, self.last_dma.get(sem), waits)
        for s, v in waits.items():
            self.waited[eng][s] = v
        if sem is None:
            s = eng
            inc = 1
        else:
            s = sem
            inc = 16
        tok = None
        if signal:
            self.cnt[s] = self.cnt.get(s, 0) + inc
            tok = (s, self.cnt[s])
            if sem is not None:
                self.last_dma[sem] = tok
            for b in reads:
                if b.r.get(s, 0) < tok[1]:
                    b.r[s] = tok[1]
            for b in writes:
                b.w = tok
                b.r = {}
        self.streams[eng].append((list(waits.items()), fn, s if signal else None, inc))
        return tok

    def group(self, eng, fns, reads=(), writes=()):
        n = len(fns)
        tok = None
        for i, fn in enumerate(fns):
            last = (i == n - 1)
            if i == 0 or last:
                tok = self.op(eng, fn, reads, writes, signal=last)
            else:
                self.streams[eng].append(([], fn, None, 1))
        return tok

    def wait_all(self, eng, bufs):
        self.op(eng, None, reads=bufs, signal=False)

    def emit(self, nc):
        names = set(self.cnt.keys())
        for e in self.ENG:
            for waits, fn, s, inc in self.streams[e]:
                for ws, wv in waits:
                    names.add(ws)
        with ExitStack() as st:
            sems = {n: st.enter_context(nc.semaphore("s_" + n)) for n in sorted(names)}
            block = st.enter_context(nc.Block())
            engmap = {"pe": block.tensor, "act": block.scalar, "dve": block.vector,
                      "pool": block.gpsimd, "sp": block.sync}
            for e in self.ENG:
                stream = self.streams[e]

                def body(engine, stream=stream):
                    for waits, fn, s, inc in stream:
                        for ws, wv in waits:
                            engine.wait_ge(sems[ws], wv)
                        if fn is None:
                            continue
                        ins = fn(engine)
                        if s is not None:
                            ins.then_inc(sems[s], inc)
                engmap[e](body)


def build_program(n_seq=2, n_tiles=4, n_layers=2, dbg=False):
    S = n_tiles * T
    nc = bass.Bass("TRN2", target_bir_lowering=False)
    P = Prog()
    xT = nc.dram_tensor("xT", [n_seq, 128, NCH, S], F32, kind="ExternalInput").ap()
    pT = nc.dram_tensor("pT", [L, n_seq, 128, 2, S], F32, kind="ExternalInput").ap()
    wpk = nc.dram_tensor("wpk", [L, NL, 128, LW], F32, kind="ExternalInput").ap()
    prm = nc.dram_tensor("prm", [128, NPRM], F32, kind="ExternalInput").ap()
    cst = nc.dram_tensor("cst", [128, NCST], F32, kind="ExternalInput").ap()
    btd = nc.dram_tensor("btd", [L, 8, 128, 640], F32, kind="ExternalInput").ap()
    outT = nc.dram_tensor("outT", [n_seq, 128, NCH, S], F32, kind="ExternalOutput").ap()
    wbf = [nc.dram_tensor("wbf%d" % l, [NL, 128, LW], BF16).ap() for l in range(L)]
    kcd = nc.dram_tensor("kcache", [L, 128, 8, T], BF16).ap()
    vcd = nc.dram_tensor("vcache", [L, 128, 4, 1024], BF16).ap()
    dbg_outs = []

    def sb(name, shape, dt=F32):
        return nc.alloc_sbuf_tensor(name, shape, dt)

    h = sb("h", [128, NCH, T]); hb = [Buf("h%d" % c) for c in range(NCH)]
    xn = sb("xn", [128, NCH, T], BF16); xb = [Buf("xn%d" % c) for c in range(NCH)]
    ring = [sb("ring%d" % i, [128, LW], BF16) for i in range(NSLOT)]
    ringb = [Buf("ring%d" % i) for i in range(NSLOT)]
    ybig = sb("ybig", [128, 24, T], BF16); yb = [Buf("y%d" % i) for i in range(24)]
    r2 = sb("r2", [128, 32 * T], BF16); r2b = [Buf("r2_%d" % i) for i in range(32)]
    KT = r2[:, 0:16 * T].rearrange("p (h k) -> p h k", h=8)
    Vt = r2[:, 16 * T:32 * T].rearrange("p (b f) -> p b f", b=8)
    merged = r2[:, 0:16 * T].rearrange("p (c t) -> p c t", c=16)
    abuf = r2[:, :].rearrange("p (c t) -> p c t", c=32)
    ostage = r2[:, :].bitcast(F32).rearrange("p (c t) -> p c t", c=16)

    def ktb(hh, half):
        return r2b[hh * 2 + half]

    def vtb(kb):
        return [r2b[16 + 2 * kb], r2b[17 + 2 * kb]]
    prm_s = sb("prm_s", [128, NPRM]); prmb = Buf("prm")
    cst_s = sb("cst_s", [128, NCST]); cstb = Buf("cst")
    ident = sb("ident", [128, 128], BF16)
    ones = sb("ones", [128, 128], BF16)
    mask2 = cst_s[:, 256:384]
    resetm = cst_s[:, 384:896]
    lb_t = sb("lb_t", [128, L, 8]); oml_t = sb("oml_t", [128, L, 8])
    rstd = sb("rstd", [128, T]); rstdb = Buf("rstd")
    S_st = [sb("S%d" % l, [128, 8, 128]) for l in range(L)]
    Sx = sb("Sx", [128, 3, 128]); Sxb = [Buf() for _ in range(3)]
    Sb = [[Buf("S%d_%d" % (l, hh)) for hh in range(8)] for l in range(L)]
    ctail = [sb("ctail%d" % l, [128, 8, 2]) for l in range(L)]
    ctb = [[Buf() for j in range(8)] for l in range(L)]
    c_t1 = sb("c_t1", [128, T]); c_t1b = Buf()
    c_u = sb("c_u", [128, T + 2]); c_ub = Buf()
    c_acc = sb("c_acc", [128, T]); c_accb = Buf()
    qT = [sb("qT%d" % i, [128, T], BF16) for i in range(2)]; qTb = [Buf(), Buf()]
    BTs = [sb("BT%d" % i, [128, 640], BF16) for i in range(2)]; BTb = [Buf(), Buf()]
    a_pt = [sb("a_pt%d" % i, [128, T], BF16) for i in range(4)]; a_ptb = [Buf() for _ in range(4)]
    a_rd = sb("a_rd", [128, T]); a_rdb = Buf()
    NT_ = 6
    ht = [sb("ht%d" % i, [128, T]) for i in range(NT_)]; htb = [Buf("ht%d" % i) for i in range(NT_)]
    h_q = sb("h_q", [128, T], BF16); h_qb = Buf()
    h_k = sb("h_k", [128, T], BF16); h_kb = Buf()
    h_qd = sb("h_qd", [128, T]); h_qdb = Buf()
    h_kh = sb("h_kh", [128, T], BF16); h_khb = Buf()
    h_khT = sb("h_khT", [128, 4, 128], BF16); h_khTb = [Buf() for _ in range(4)]
    h_at = [sb("h_at%d" % i, [128, 128], BF16) for i in range(4)]; h_atb = [Buf() for _ in range(4)]
    h_dec = sb("h_dec", [128, 8]); h_decb = Buf()
    h_gs = sb("h_gs", [128, T]); h_gsb = Buf()
    h_sq = sb("h_sq", [128, T], BF16); h_sqb = Buf()
    m_e = [sb("m_e%d" % i, [128, T]) for i in range(2)]; m_eb = [Buf(), Buf()]
    m_acc = sb("m_acc", [128, T]); m_accb = Buf()
    f_r = [sb("f_r%d" % i, [128, T]) for i in range(2)]; f_rb = [Buf(), Buf()]
    pbf = sb("pbf", [128, 2, T], BF16); pbfb = Buf()

    ps = [nc.alloc_psum_tensor("ps%d" % i, [128, T], F32) for i in range(8)]
    psb = [Buf("ps%d" % i) for i in range(8)]
    rot = [0]

    nrot = [6]

    def nbank():
        i = rot[0] % nrot[0]
        rot[0] = (i + 1) % nrot[0]
        return i
    rot2 = [0]

    def nsmall():
        i = 4 + rot2[0]
        rot2[0] = (rot2[0] + 1) % 2
        return i
    qsl = [(4, 0), (5, 0)]
    qslb = [psb[4], psb[5]]
    rotq = [0]

    def nquarter():
        i = rotq[0]
        rotq[0] = (rotq[0] + 1) % 2
        return i

    P.op("sp", lambda e: e.dma_start(out=prm_s[:], in_=prm[:, :]), writes=[prmb], sem="ldp")
    P.op("sp", lambda e: e.dma_start(out=cst_s[:], in_=cst[:, :]), writes=[cstb], sem="ldc")
    identb = Buf(); onesb = Buf(); lbb = Buf()
    P.op("dve", lambda e: e.tensor_copy(out=ident[:], in_=cst_s[:, 0:128]), reads=[cstb], writes=[identb])
    P.op("dve", lambda e: e.tensor_copy(out=ones[:], in_=cst_s[:, 128:256]), reads=[cstb], writes=[onesb])
    P.op("dve", lambda e: e.memset(lb_t[:, 0, :], 0.0), writes=[lbb])
    P.op("dve", lambda e: e.memset(oml_t[:, 0, :], 1.0), reads=[lbb], writes=[lbb])
    l0 = prm_s[:, 72:80]
    l1 = prm_s[:, PRM_L + 72:PRM_L + 80]
    P.op("dve", lambda e: e.tensor_tensor(out=lb_t[:, 1, :], in0=l0, in1=l1, op=ALU.subtract), reads=[prmb, lbb], writes=[lbb])
    P.op("act", lambda e: e.activation(out=oml_t[:, 1, :], in_=lb_t[:, 1, :], func=AF.Exp), reads=[lbb], writes=[lbb])
    P.op("dve", lambda e: e.tensor_scalar(out=lb_t[:, 1, :], in0=oml_t[:, 1, :], scalar1=1.0, scalar2=None, op0=ALU.add), reads=[lbb], writes=[lbb])
    P.op("dve", lambda e: e.reciprocal(out=lb_t[:, 1, :], in_=lb_t[:, 1, :]), reads=[lbb], writes=[lbb])
    P.op("dve", lambda e: e.tensor_tensor(out=oml_t[:, 1, :], in0=oml_t[:, 1, :], in1=lb_t[:, 1, :], op=ALU.mult), reads=[lbb], writes=[lbb])

    wbfb = [[Buf("wbf%d_%d" % (l, i)) for i in range(NL)] for l in range(L)]
    gload = [0]
    wtiles = {}

    class WS:
        def __init__(self, l, first_use):
            self.l = l
            self.first_use = first_use
            self.pos = 0
            self.cur = -1
            self.tiles = []
            self.cur_slot = None

        def tile(self, mat, kc, col, w):
            off = self.pos % LW
            if off + w > LW:
                self.pos += LW - off
            ld = self.pos // LW
            off = self.pos % LW
            self.tiles.append((mat, kc, col, w, ld, off))
            self.pos += w
            if ld > self.cur:
                assert ld == self.cur + 1 and ld < NL
                slot = gload[0] % NSLOT
                gload[0] += 1
                l = self.l
                if self.first_use:
                    P.op("pool", lambda e, slot=slot, l=l, ld=ld: e.dma_start(out=ring[slot][:], in_=wpk[l, ld]),
                         writes=[ringb[slot]], sem="wq%d" % slot)
                    P.op("sp", lambda e, slot=slot, l=l, ld=ld: e.dma_start(out=wbf[l][ld], in_=ring[slot][:]),
                         reads=[ringb[slot]], writes=[wbfb[l][ld]], sem="ws%d" % slot)
                else:
                    P.op("sp", lambda e, slot=slot, l=l, ld=ld: e.dma_start(out=ring[slot][:], in_=wbf[l][ld]),
                         reads=[wbfb[l][ld]], writes=[ringb[slot]], sem="w%d" % slot)
                self.cur = ld
                self.cur_slot = slot
            return ring[self.cur_slot][:, off:off + w], ringb[self.cur_slot]

    def proj_group(ws, mat, col, nk, rhs_of, bank, extra_reads):
        pairs = []
        rb = set()
        for kc in range(nk):
            wt, wb = ws.tile(mat, kc, col, 128)
            pairs.append((wt, rhs_of(kc)))
            rb.add(wb)
        n = len(pairs)
        fns = [(lambda e, a=a, b=b, i=i: e.matmul(ps[bank][:, :], lhsT=a, rhs=b, start=(i == 0), stop=(i == n - 1)))
               for i, (a, b) in enumerate(pairs)]
        return P.group("pe", fns, reads=list(rb) + list(extra_reads), writes=[psb[bank]])

    def proj_pieces(ws, mat, col, nk, rhs_of, rhs_bufs_of, bank, piece=4):
        for p0 in range(0, nk, piece):
            pairs = []
            rb = []
            for kc in range(p0, min(nk, p0 + piece)):
                wt, wb = ws.tile(mat, kc, col, 128)
                pairs.append((wt, rhs_of(kc), kc))
                rb.append(wb)
                rb.extend(rhs_bufs_of(kc))
            fns = [(lambda e, a=a, b=b, kc=kc: e.matmul(ps[bank][:, :], lhsT=a, rhs=b, start=(kc == 0), stop=(kc == nk - 1)))
                   for (a, b, kc) in pairs]
            P.group("pe", fns, reads=rb, writes=[psb[bank]])
            yield

    def proj_multi(ws, specs):
        banks = [nbank() for _ in specs]
        for kc in range(NCH):
            fns = []
            rb = [xb[kc]]
            for (mat, col), bank in zip(specs, banks):
                wt, wb = ws.tile(mat, kc, col, 128)
                rb.append(wb)
                fns.append(lambda e, wt=wt, bank=bank, kc=kc: e.matmul(ps[bank][:, :], lhsT=wt, rhs=xn[:, kc, :], start=(kc == 0), stop=(kc == NCH - 1)))
            P.group("pe", fns, reads=rb, writes=[psb[b] for b in banks])
        return banks

    def vproj(ws, col0, dst_of, dst_bufs_of):
        for half in range(2):
            fns = []
            rb = set()
            for kc in range(NCH):
                wt, wb = ws.tile("w_in", kc, col0 + half * 512, 512)
                rb.add(wb)
                for tb in range(4):
                    fns.append(lambda e, wt=wt, kc=kc, tb=tb: e.matmul(
                        ps[tb][:, :], lhsT=xn[:, kc, tb * 128:(tb + 1) * 128], rhs=wt,
                        start=(kc == 0), stop=(kc == NCH - 1)))
            P.group("pe", fns, reads=list(rb) + xb, writes=psb[0:4])
            for tb in range(4):
                eng = "act" if tb % 2 == 0 else "dve"
                dst = dst_of(tb)[:, half * 512:(half + 1) * 512]
                if eng == "act":
                    P.op("act", lambda e, dst=dst, tb=tb: e.activation(out=dst, in_=ps[tb][:, :], func=AF.Copy),
                         reads=[psb[tb]], writes=[dst_bufs_of(tb)[half]])
                else:
                    P.op("dve", lambda e, dst=dst, tb=tb: e.tensor_copy(out=dst, in_=ps[tb][:, :]),
                         reads=[psb[tb]], writes=[dst_bufs_of(tb)[half]])
        rot[0] = 0

    def rmsnorm(gcol, dst, dst_bufs, dst_is_xn=True):
        for q in range(4):
            if q % 2 == 0:
                P.op("act", lambda e, q=q: e.activation(out=xn[:, 4 * q:4 * q + 4, :], in_=h[:, 4 * q:4 * q + 4, :],
                                                        func=AF.Square, scale=float(D ** -0.5)),
                     reads=hb[4 * q:4 * q + 4], writes=xb[4 * q:4 * q + 4])
            else:
                P.op("dve", lambda e, q=q: e.scalar_tensor_tensor(out=xn[:, 4 * q:4 * q + 4, :], in0=h[:, 4 * q:4 * q + 4, :], scalar=float(1.0 / D),
                                                                  in1=h[:, 4 * q:4 * q + 4, :], op0=ALU.mult, op1=ALU.mult),
                     reads=hb[4 * q:4 * q + 4], writes=xb[4 * q:4 * q + 4])
        for q in (0, 2, 1, 3):
            pass
        order = (0, 1, 2, 3)
        for qi_, q in enumerate(order):
            fns = [(lambda e, c=c, qi_=qi_: e.matmul(ps[7][:, :], lhsT=ones[:, :], rhs=xn[:, c, :], start=(qi_ == 0 and c % 4 == 0), stop=(qi_ == 3 and c % 4 == 3)))
                   for c in range(4 * q, 4 * q + 4)]
            P.group("pe", fns, reads=xb[4 * q:4 * q + 4] + [onesb], writes=[psb[7]])
        P.op("act", lambda e: e.activation(out=rstd[:, :], in_=ps[7][:, :], func=AF.Ln, bias=EPS, scale=1.0),
             reads=[psb[7]], writes=[rstdb])
        P.op("act", lambda e: e.activation(out=rstd[:, :], in_=rstd[:, :], func=AF.Exp, scale=-0.5),
             reads=[rstdb], writes=[rstdb])
        for c in range(NCH):
            P.op("dve", lambda e, c=c: e.scalar_tensor_tensor(out=dst[:, c, :], in0=h[:, c, :],
                                                              scalar=prm_s[:, gcol + c:gcol + c + 1], in1=rstd[:, :],
                                                              op0=ALU.mult, op1=ALU.mult),
                 reads=[hb[c], rstdb, prmb], writes=dst_bufs(c))

    def dump(name, ap, bufs, shape):
        if not dbg:
            return
        o = nc.dram_tensor(name, shape, F32, kind="ExternalOutput").ap()
        ob = Buf()
        P.op("pool", lambda e: e.dma_start(out=o, in_=ap), reads=bufs, writes=[ob], sem="dbg_" + name)
        dbg_outs.append(ob)

    outb = Buf("out")
    xldb = Buf()
    kcb = [Buf() for _ in range(L)]
    vcb = [Buf() for _ in range(L)]

    for sq in range(n_seq):
        for ti in range(n_tiles):
            t0 = ti * T
            first = (ti == 0)
            P.op(DMAQ, lambda e, sq=sq, t0=t0: e.dma_start(out=h[:, :, :], in_=xT[sq, :, :, t0:t0 + T]),
                 writes=hb, sem="ldx")
            for l in range(n_layers):
                ws = WS(l, sq == 0 and ti == 0)
                pr = l * PRM_L
                rmsnorm(pr + 0, xn, lambda c: [xb[c]])
                if not first:
                    P.op(DMAQ, lambda e, l=l: e.dma_start(out=KT[:, :, 0:T], in_=kcd[l]),
                         reads=[kcb[l]], writes=[ktb(hh, 0) for hh in range(8)], sem="ldk")
                    P.op(DMAQ, lambda e, l=l: e.dma_start(out=Vt[:, 0:4, :], in_=vcd[l]),
                         reads=[vcb[l]], writes=[b for kb in range(4) for b in vtb(kb)], sem="ldv")
                for j in range(8):
                    bk = {}
                    if j == 0:
                        mb = proj_multi(ws, [("w_in", COL[part]) for part in ("cb", "cc", "ch")])
                        bk = dict(zip(("cb", "cc", "ch"), mb))
                    else:
                        for part in ("cb", "cc", "ch"):
                            bk[part] = nbank()
                            proj_group(ws, "w_in", COL[part] + j * 128, NCH, lambda kc: xn[:, kc, :], bk[part], xb)
                    P.op("act", lambda e, b=bk["cc"]: e.activation(out=c_t1[:, :], in_=ps[b][:, :], func=AF.Copy),
                         reads=[psb[bk["cc"]]], writes=[c_t1b])
                    if first:
                        P.op("dve", lambda e: e.memset(c_u[:, 0:2], 0.0), writes=[c_ub])
                    else:
                        P.op("act", lambda e, l=l, j=j: e.activation(out=c_u[:, 0:2], in_=ctail[l][:, j, :], func=AF.Copy),
                             reads=[ctb[l][j]], writes=[c_ub])
                    P.op("dve", lambda e, b=bk["ch"]: e.tensor_tensor(out=c_u[:, 2:T + 2], in0=ps[b][:, :], in1=c_t1[:, :], op=ALU.mult),
                         reads=[psb[bk["ch"]], c_t1b, c_ub], writes=[c_ub])
                    P.op("act", lambda e, l=l, j=j: e.activation(out=ctail[l][:, j, :], in_=c_u[:, T:T + 2], func=AF.Copy),
                         reads=[c_ub], writes=[ctb[l][j]])
                    cw = pr + 48 + j * 3
                    P.op("dve", lambda e, cw=cw: e.tensor_scalar(out=c_acc[:, :], in0=c_u[:, 0:T], scalar1=prm_s[:, cw:cw + 1], scalar2=None, op0=ALU.mult),
                         reads=[c_ub, prmb], writes=[c_accb])
                    P.op("dve", lambda e, cw=cw: e.scalar_tensor_tensor(out=c_acc[:, :], in0=c_u[:, 1:T + 1], scalar=prm_s[:, cw + 1:cw + 2], in1=c_acc[:, :], op0=ALU.mult, op1=ALU.add),
                         reads=[c_ub, c_accb], writes=[c_accb])
                    P.op("dve", lambda e, cw=cw: e.scalar_tensor_tensor(out=c_acc[:, :], in0=c_u[:, 2:T + 2], scalar=prm_s[:, cw + 2:cw + 3], in1=c_acc[:, :], op0=ALU.mult, op1=ALU.add),
                         reads=[c_ub, c_accb], writes=[c_accb])
                    P.op("dve", lambda e, b=bk["cb"], j=j: e.tensor_tensor(out=ybig[:, j, :], in0=ps[b][:, :], in1=c_acc[:, :], op=ALU.mult),
                         reads=[psb[bk["cb"]], c_accb], writes=[yb[j]])
                nrot[0] = 4
                vproj(ws, COL["av"], lambda tb: Vt[:, 4 + tb, :], lambda tb: vtb(4 + tb))
                nrot[0] = 3
                sbanks = (3, 4, 5)
                DEPTH = 3
                def att_proj(hh):
                    qi = hh % 2
                    P.op("pool", lambda e, l=l, hh=hh, qi=qi: e.dma_start(out=BTs[qi][:, :], in_=btd[l, hh]),
                         writes=[BTb[qi]], sem="ldb%d" % qi)
                    bq = nbank()
                    yield from proj_pieces(ws, "w_in", COL["aq"] + hh * 128, NCH, lambda kc: xn[:, kc, :], lambda kc: [xb[kc]], bq)
                    P.op("act", lambda e, bq=bq, qi=qi: e.activation(out=qT[qi][:, :], in_=ps[bq][:, :], func=AF.Copy, scale=float(128 ** -0.5)),
                         reads=[psb[bq]], writes=[qTb[qi]])
                    bkk = nbank()
                    yield from proj_pieces(ws, "w_in", COL["ak"] + hh * 128, NCH, lambda kc: xn[:, kc, :], lambda kc: [xb[kc]], bkk)
                    P.op("dve", lambda e, bkk=bkk, hh=hh: e.tensor_copy(out=KT[:, hh, T:2 * T], in_=ps[bkk][:, :]),
                         reads=[psb[bkk]], writes=[ktb(hh, 1)])

                def att_chain(hh, filler):
                    qi = hh % 2
                    kbs = [4, 5, 6, 7] if first else [3, 4, 0, 1, 2, 5, 6, 7]
                    nk = len(kbs)
                    geo = []
                    for kb in kbs:
                        lo = max(0, 2 * kb - 8)
                        hi = min(7, 2 * kb + 1)
                        geo.append((kb, (hi - lo + 1) * 64, lo * 64, (lo + 8 - 2 * kb) * 64))

                    def att_front(ii):
                        kb, n, c0, e0 = geo[ii]
                        sbk = sbanks[ii % 3]
                        pi = ii % 4
                        kt_ap = KT[:, hh, kb * 128:(kb + 1) * 128]
                        q_ap = qT[qi][:, c0:c0 + n]
                        bt_ap = BTs[qi][:, e0:e0 + n]
                        P.group("pe", [
                            lambda e, sbk=sbk, n=n, kt_ap=kt_ap, q_ap=q_ap: e.matmul(ps[sbk][:, 0:n], lhsT=kt_ap, rhs=q_ap, start=True, stop=False),
                            lambda e, sbk=sbk, n=n, bt_ap=bt_ap: e.matmul(ps[sbk][:, 0:n], lhsT=ident[:, :], rhs=bt_ap, start=False, stop=True)],
                            reads=[ktb(hh, kb // 4), qTb[qi], BTb[qi], identb], writes=[psb[sbk]])
                        P.op("act", lambda e, sbk=sbk, pi=pi, n=n: e.activation(out=a_pt[pi][:, 0:n], in_=ps[sbk][:, 0:n], func=AF.Exp),
                             reads=[psb[sbk]], writes=[a_ptb[pi]])

                    def att_back(ii):
                        kb, n, c0, e0 = geo[ii]
                        pi = ii % 4
                        st_ = (ii == 0)
                        sp_ = (ii == nk - 1)
                        v_ap = Vt[:, kb, hh * 128:(hh + 1) * 128]
                        P.op("pe", lambda e, pi=pi, c0=c0, n=n, st_=st_, sp_=sp_: e.matmul(ps[7][:, c0:c0 + n], lhsT=ones[:, :], rhs=a_pt[pi][:, 0:n], start=st_, stop=sp_),
                             reads=[a_ptb[pi], onesb], writes=[psb[7]])
                        P.op("pe", lambda e, pi=pi, c0=c0, n=n, st_=st_, sp_=sp_, v_ap=v_ap: e.matmul(ps[6][:, c0:c0 + n], lhsT=v_ap, rhs=a_pt[pi][:, 0:n], start=st_, stop=sp_),
                             reads=[a_ptb[pi]] + vtb(kb), writes=[psb[6]])
                    for ii in range(nk):
                        att_front(ii)
                        if ii >= DEPTH - 1:
                            att_back(ii - (DEPTH - 1))
                        next(filler, None)
                    for ii in range(max(0, nk - (DEPTH - 1)), nk):
                        att_back(ii)
                        next(filler, None)
                    for _ in filler:
                        pass
                    P.op("dve", lambda e: e.reciprocal(out=a_rd[:, :], in_=ps[7][:, :]), reads=[psb[7]], writes=[a_rdb])
                    P.op("dve", lambda e, hh=hh: e.tensor_tensor(out=ybig[:, 16 + hh, :], in0=ps[6][:, :], in1=a_rd[:, :], op=ALU.mult),
                         reads=[psb[6], a_rdb], writes=[yb[16 + hh]])
                if N_ATT > 0:
                    for _ in att_proj(0):
                        pass
                for hh in range(N_ATT):
                    att_chain(hh, att_proj(hh + 1) if hh + 1 < N_ATT else iter(()))
                nrot[0] = 4
                if ti < n_tiles - 1:
                    P.op(DMAQ, lambda e, l=l: e.dma_start(out=kcd[l], in_=KT[:, :, T:2 * T]),
                         reads=[ktb(hh, 1) for hh in range(8)], writes=[kcb[l]], sem="stk")
                    P.op(DMAQ, lambda e, l=l: e.dma_start(out=vcd[l], in_=Vt[:, 4:8, :]),
                         reads=[b for kb in range(4, 8) for b in vtb(kb)], writes=[vcb[l]], sem="stv")
                vproj(ws, COL["hi"], lambda tb: Vt[:, tb, :], lambda tb: vtb(tb))
                if first:
                    P.op("dve", lambda e, l=l: e.memset(S_st[l][:, :, :], 0.0), writes=Sb[l])
                def hg_proj(hh):
                    bq = nbank(); proj_group(ws, "w_in", COL["hq"] + hh * 128, NCH, lambda kc: xn[:, kc, :], bq, xb)
                    bf = nbank(); proj_group(ws, "w_in", COL["hf"] + hh * 128, NCH, lambda kc: xn[:, kc, :], bf, xb)
                    bg = nbank(); proj_group(ws, "w_in", COL["hg"] + hh * 128, NCH, lambda kc: xn[:, kc, :], bg, xb)
                    return bq, bf, bg

                def hg_prep(hh, banks):
                    bq, bf, bg = banks
                    lbc = lb_t[:, l, hh:hh + 1]
                    omc = oml_t[:, l, hh:hh + 1]
                    P.op("act", lambda e: e.activation(out=ht[0][:, :], in_=ps[bf][:, :], func=AF.Exp, scale=-1.0),
                         reads=[psb[bf]], writes=[htb[0]])
                    P.op("act", lambda e: e.activation(out=h_gs[:, :], in_=ps[bg][:, :], func=AF.Copy), reads=[psb[bg]], writes=[h_gsb])
                    P.op("act", lambda e: e.activation(out=h_qd[:, :], in_=ps[bq][:, :], func=AF.Copy), reads=[psb[bq]], writes=[h_qdb])
                    P.op("act", lambda e: e.activation(out=ht[1][:, :], in_=ht[0][:, :], func=AF.Ln, bias=1.0, scale=1.0), reads=[htb[0]], writes=[htb[1]])
                    P.op("act", lambda e: e.activation(out=ht[1][:, :], in_=ht[1][:, :], func=AF.Exp, scale=-1.0), reads=[htb[1]], writes=[htb[1]])
                    P.op("dve", lambda e: e.tensor_tensor(out=ht[2][:, :], in0=ht[0][:, :], in1=ht[1][:, :], op=ALU.mult),
                         reads=[htb[0], htb[1]], writes=[htb[2]])
                    P.op("act", lambda e: e.activation(out=ht[3][:, :], in_=ht[1][:, :], func=AF.Ln, bias=lbc, scale=omc),
                         reads=[htb[1], lbb], writes=[htb[3]])
                    P.op("dve", lambda e: e.tensor_tensor_scan(out=ht[4][:, :], data0=resetm, data1=ht[3][:, :], initial=0.0, op0=ALU.mult, op1=ALU.add),
                         reads=[htb[3], cstb], writes=[htb[4]])
                    G3 = ht[4][:, :].rearrange("p (c t) -> p c t", t=64)
                    Gm3 = ht[5][:, :].rearrange("p (c t) -> p c t", t=64)
                    P.op("dve", lambda e: e.tensor_tensor(out=Gm3, in0=G3, in1=G3[:, :, 31:32].broadcast_to([128, 8, 64]), op=ALU.subtract),
                         reads=[htb[4]], writes=[htb[5]])
                    D3 = ht[3][:, :].rearrange("p (c t) -> p c t", t=64)
                    P.op("dve", lambda e: e.tensor_tensor(out=D3, in0=Gm3[:, :, 63:64].broadcast_to([128, 8, 64]), in1=Gm3, op=ALU.subtract),
                         reads=[htb[5]], writes=[htb[3]])
                    P.op("act", lambda e: e.activation(out=ht[3][:, :], in_=ht[3][:, :], func=AF.Exp), reads=[htb[3]], writes=[htb[3]])
                    P.op("dve", lambda e: e.scalar_tensor_tensor(out=h_kh[:, :], in0=ht[2][:, :], scalar=omc, in1=ht[3][:, :], op0=ALU.mult, op1=ALU.mult),
                         reads=[htb[2], htb[3], lbb], writes=[h_khb])
                    P.op("act", lambda e: e.activation(out=ht[1][:, :], in_=ht[5][:, :], func=AF.Exp, scale=-1.0), reads=[htb[5]], writes=[htb[1]])
                    P.op("dve", lambda e: e.scalar_tensor_tensor(out=h_k[:, :], in0=ht[2][:, :], scalar=omc, in1=ht[1][:, :], op0=ALU.mult, op1=ALU.mult),
                         reads=[htb[2], htb[1], lbb], writes=[h_kb])
                    P.op("act", lambda e: e.activation(out=ht[0][:, :], in_=ht[5][:, :], func=AF.Exp), reads=[htb[5]], writes=[htb[0]])
                    P.op("dve", lambda e: e.tensor_tensor(out=h_q[:, :], in0=h_qd[:, :], in1=ht[0][:, :], op=ALU.mult),
                         reads=[h_qdb, htb[0]], writes=[h_qb])
                    P.op("act", lambda e: e.activation(out=h_dec[:, :], in_=G3[:, :, 63], func=AF.Exp),
                         reads=[htb[4]], writes=[h_decb])
                    P.op("act", lambda e: e.activation(out=ht[0][:, :], in_=ht[4][:, :], func=AF.Exp), reads=[htb[4]], writes=[htb[0]])
                    P.op("dve", lambda e: e.tensor_tensor(out=h_qd[:, :], in0=h_qd[:, :], in1=ht[0][:, :], op=ALU.mult),
                         reads=[h_qdb, htb[0]], writes=[h_qdb])
                    P.op("act", lambda e: e.activation(out=ht[1][:, :], in_=h_gs[:, :], func=AF.Exp, scale=-1.0), reads=[h_gsb], writes=[htb[1]])
                    P.op("act", lambda e: e.activation(out=ht[1][:, :], in_=ht[1][:, :], func=AF.Ln, bias=1.0, scale=1.0), reads=[htb[1]], writes=[htb[1]])
                    P.op("act", lambda e: e.activation(out=ht[1][:, :], in_=ht[1][:, :], func=AF.Exp, scale=-1.0), reads=[htb[1]], writes=[htb[1]])
                    P.op("dve", lambda e: e.tensor_tensor(out=h_gs[:, :], in0=h_gs[:, :], in1=ht[1][:, :], op=ALU.mult),
                         reads=[h_gsb, htb[1]], writes=[h_gsb])

                def hg_smalls(hh):
                    for tb in range(4):
                        tsl = slice(tb * 128, (tb + 1) * 128)
                        bk = nsmall()
                        P.op("pe", lambda e, bk=bk, tsl=tsl: e.matmul(ps[bk][:, 0:128], lhsT=h_kh[:, tsl], rhs=ident[:, :], start=True, stop=True),
                             reads=[h_khb, identb], writes=[psb[bk]])
                        P.op("act", lambda e, bk=bk, tb=tb: e.activation(out=h_khT[:, tb, :], in_=ps[bk][:, 0:128], func=AF.Copy),
                             reads=[psb[bk]], writes=[h_khTb[tb]])
                    for tb in range(4):
                        tsl = slice(tb * 128, (tb + 1) * 128)
                        bk = nsmall()
                        P.op("pe", lambda e, bk=bk, tsl=tsl: e.matmul(ps[bk][:, 0:128], lhsT=h_k[:, tsl], rhs=h_q[:, tsl], start=True, stop=True),
                             reads=[h_kb, h_qb], writes=[psb[bk]])
                        P.op("dve", lambda e, bk=bk, tb=tb: e.tensor_tensor(out=h_at[tb][:, :], in0=ps[bk][:, 0:128], in1=mask2, op=ALU.mult),
                             reads=[psb[bk], cstb], writes=[h_atb[tb]])
                    n_ds = 7 if ti == n_tiles - 1 else 8
                    for c in range(n_ds):
                        bank = 4 + c % 2
                        off = (c // 2) * 128
                        tb = c // 2
                        rs = slice((c % 2) * 64, (c % 2) * 64 + 64)
                        v_ap = Vt[rs, tb, hh * 128:(hh + 1) * 128]
                        P.op("pe", lambda e, bank=bank, off=off, rs=rs, tb=tb, v_ap=v_ap: e.matmul(
                            ps[bank][:, off:off + 128], lhsT=h_khT[rs, tb, :], rhs=v_ap, start=True, stop=True),
                            reads=[h_khTb[tb]] + vtb(tb), writes=[psb[bank]])

                    def stbuf(c):
                        if c == 0:
                            return S_st[l][:, hh, :], Sb[l][hh]
                        return Sx[:, (c - 1) % 3, :], Sxb[(c - 1) % 3]
                    for tb in range(4):
                        tsl = slice(tb * 128, (tb + 1) * 128)
                        v_ap = Vt[:, tb, hh * 128:(hh + 1) * 128]
                        P.op("pe", lambda e, tb=tb, tsl=tsl, v_ap=v_ap: e.matmul(ps[6][:, tsl], lhsT=v_ap, rhs=h_at[tb][:, :], start=True, stop=False),
                             reads=[h_atb[tb]] + vtb(tb), writes=[psb[6]])
                        for cc in range(2):
                            c = 2 * tb + cc
                            csl = slice(c * 64, (c + 1) * 64)
                            src, srcb = stbuf(c)
                            if not HG_NO_INTER:
                                P.op("pe", lambda e, csl=csl, src=src, cc=cc: e.matmul(ps[6][:, csl], lhsT=src, rhs=h_qd[:, csl], start=False, stop=(cc == 1)),
                                     reads=[srcb, h_qdb], writes=[psb[6]])
                            if c < n_ds and not HG_NO_CHAIN:
                                bank = 4 + c % 2
                                off = (c // 2) * 128
                                if c == 7:
                                    dst, dstb = S_st[l][:, hh, :], Sb[l][hh]
                                else:
                                    dst, dstb = stbuf(c + 1)
                                P.op("dve", lambda e, bank=bank, off=off, src=src, dst=dst, c=c: e.scalar_tensor_tensor(
                                    out=dst, in0=src, scalar=h_dec[:, c:c + 1], in1=ps[bank][:, off:off + 128], op0=ALU.mult, op1=ALU.add),
                                    reads=[srcb, h_decb, psb[bank]], writes=[dstb])
                    P.op("act", lambda e: e.activation(out=h_sq[:, :], in_=ps[6][:, :], func=AF.Square, scale=float(128 ** -0.5)),
                         reads=[psb[6]], writes=[h_sqb])
                    P.op("pe", lambda e: e.matmul(ps[7][:, :], lhsT=ones[:, :], rhs=h_sq[:, :], start=True, stop=True),
                         reads=[h_sqb, onesb], writes=[psb[7]])
                    P.op("act", lambda e: e.activation(out=ht[0][:, :], in_=ps[7][:, :], func=AF.Ln, bias=EPS, scale=1.0), reads=[psb[7]], writes=[htb[0]])
                    P.op("act", lambda e: e.activation(out=ht[0][:, :], in_=ht[0][:, :], func=AF.Exp, scale=-0.5), reads=[htb[0]], writes=[htb[0]])
                    P.op("dve", lambda e: e.tensor_tensor(out=ht[1][:, :], in0=ps[6][:, :], in1=ht[0][:, :], op=ALU.mult),
                         reads=[psb[6], htb[0]], writes=[htb[1]])
                    ngc = prm_s[:, pr + 80:pr + 81]
                    P.op("dve", lambda e, hh=hh, ngc=ngc: e.scalar_tensor_tensor(out=ybig[:, 8 + hh, :], in0=ht[1][:, :], scalar=ngc, in1=h_gs[:, :], op0=ALU.mult, op1=ALU.mult),
                         reads=[htb[1], h_gsb, prmb], writes=[yb[8 + hh]])
                nrot[0] = 4
                if N_HG > 0:
                    hg_banks = hg_proj(0)
                for hh in range(N_HG):
                    hg_prep(hh, hg_banks)
                    if hh + 1 < N_HG:
                        hg_banks = hg_proj(hh + 1)
                    hg_smalls(hh)
                if dbg and sq == 0 and ti == dbg_tile and l == dbg_layer:
                    for nm, a_, b_ in (("d_yconv", 0, 8), ("d_yhg", 8, 16), ("d_yatt", 16, 24)):
                        dump(nm, ybig[:, a_:b_, :], yb[a_:b_], [128, 8, T])
                nrot[0] = 6
                for j in range(NCH):
                    for br, gname in enumerate(("ga", "gb", "gc")):
                        bgt = nbank()
                        proj_group(ws, "w_in", COL[gname] + j * 128, NCH, lambda kc: xn[:, kc, :], bgt, xb)
                        bro = nbank()
                        yoff = (0, 8, 16)[br]
                        proj_group(ws, "w_br%d" % br, j * 128, 8, lambda kc, yoff=yoff: ybig[:, yoff + kc, :], bro, yb[yoff:yoff + 8])
                        mi = br % 2
                        P.op("act", lambda e, bgt=bgt, mi=mi: e.activation(out=m_e[mi][:, :], in_=ps[bgt][:, :], func=AF.Sigmoid),
                             reads=[psb[bgt]], writes=[m_eb[mi]])
                        if br == 0:
                            P.op("dve", lambda e, bro=bro, mi=mi: e.tensor_tensor(out=m_acc[:, :], in0=ps[bro][:, :], in1=m_e[mi][:, :], op=ALU.mult),
                                 reads=[psb[bro], m_eb[mi]], writes=[m_accb])
                        else:
                            P.op("dve", lambda e, bro=bro, mi=mi: e.tensor_tensor(out=m_e[mi][:, :], in0=ps[bro][:, :], in1=m_e[mi][:, :], op=ALU.mult),
                                 reads=[psb[bro], m_eb[mi]], writes=[m_eb[mi]])
                            if br == 1:
                                P.op("dve", lambda e, mi=mi: e.tensor_tensor(out=m_acc[:, :], in0=m_acc[:, :], in1=m_e[mi][:, :], op=ALU.add),
                                     reads=[m_accb, m_eb[mi]], writes=[m_accb])
                            else:
                                P.op("dve", lambda e, mi=mi, j=j: e.tensor_tensor(out=merged[:, j, :], in0=m_acc[:, :], in1=m_e[mi][:, :], op=ALU.add),
                                     reads=[m_accb, m_eb[mi]], writes=[r2b[j]])
                for j in range(NCH):
                    bo = nbank()
                    proj_group(ws, "w_o", j * 128, NCH, lambda kc: merged[:, kc, :], bo, r2b[0:16])
                    P.op("dve", lambda e, bo=bo, j=j: e.tensor_tensor(out=h[:, j, :], in0=h[:, j, :], in1=ps[bo][:, :], op=ALU.add),
                         reads=[psb[bo]], writes=[hb[j]])
                if dbg and sq == 0 and ti == dbg_tile and l == dbg_layer:
                    dump("d_hmix", h[:, :, :], hb, [128, NCH, T])
                rmsnorm(pr + 16, xn, lambda c: [xb[c]])
                for half in range(2):
                    pre = {}
                    if half == 0:
                        mb = proj_multi(ws, [("w_ff1", c * 128) for c in range(4)])
                        pre = dict(zip(range(4), mb))
                    for c in range(32):
                        if c in pre:
                            bf1 = pre[c]
                        else:
                            bf1 = nbank()
                            proj_group(ws, "w_ff1", (half * 32 + c) * 128, NCH, lambda kc: xn[:, kc, :], bf1, xb)
                        fi = c % 2
                        P.op("act", lambda e, bf1=bf1, fi=fi: e.activation(out=f_r[fi][:, :], in_=ps[bf1][:, :], func=AF.Relu),
                             reads=[psb[bf1]], writes=[f_rb[fi]])
                        P.op("dve", lambda e, fi=fi, c=c: e.tensor_tensor(out=abuf[:, c, :], in0=f_r[fi][:, :], in1=f_r[fi][:, :], op=ALU.mult),
                             reads=[f_rb[fi]], writes=[r2b[c]])
                    for j in range(NCH):
                        b2 = nbank()
                        proj_group(ws, "w_ff2_%d" % half, j * 128, 32, lambda kc: abuf[:, kc, :], b2, r2b)
                        P.op("dve", lambda e, b2=b2, j=j: e.tensor_tensor(out=h[:, j, :], in0=h[:, j, :], in1=ps[b2][:, :], op=ALU.add),
                             reads=[psb[b2]], writes=[hb[j]])
                rmsnorm(pr + 32, xn, lambda c: [xb[c]])
                P.op("pool", lambda e, l=l, sq=sq, t0=t0: e.dma_start(out=pbf[:, :, :], in_=pT[l, sq, :, :, t0:t0 + T]),
                     writes=[pbfb], sem="ldpt")
                for j in range(NCH):
                    bgt = nbank()
                    proj_group(ws, "w_pg", j * 128, NCH, lambda kc: xn[:, kc, :], bgt, xb)
                    bpp = nbank()
                    proj_group(ws, "w_pi", j * 128, 2, lambda kc: pbf[:, kc, :], bpp, [pbfb])
                    mi = j % 2
                    P.op("act", lambda e, bgt=bgt, mi=mi: e.activation(out=m_e[mi][:, :], in_=ps[bgt][:, :], func=AF.Sigmoid),
                         reads=[psb[bgt]], writes=[m_eb[mi]])
                    P.op("dve", lambda e, bpp=bpp, mi=mi: e.tensor_tensor(out=m_e[mi][:, :], in0=ps[bpp][:, :], in1=m_e[mi][:, :], op=ALU.mult),
                         reads=[psb[bpp], m_eb[mi]], writes=[m_eb[mi]])
                    P.op("dve", lambda e, mi=mi, j=j: e.tensor_tensor(out=h[:, j, :], in0=h[:, j, :], in1=m_e[mi][:, :], op=ALU.add),
                         reads=[m_eb[mi]], writes=[hb[j]])
                if l not in wtiles:
                    wtiles[l] = ws.tiles
                else:
                    assert len(wtiles[l]) == len(ws.tiles) and wtiles[l][-1] == ws.tiles[-1]
                if dbg and sq == 0 and ti == dbg_tile and l == dbg_layer:
                    dump("d_hout", h[:, :, :], hb, [128, NCH, T])
            rmsnorm(2 * PRM_L, ostage, lambda c: [r2b[2 * c], r2b[2 * c + 1]])
            P.op(DMAQ, lambda e, sq=sq, t0=t0: e.dma_start(out=outT[sq, :, :, t0:t0 + T], in_=ostage),
                 reads=r2b, writes=[outb], sem="sto")
    P.wait_all(DMAQ, [outb])
    P.wait_all("pool", dbg_outs)
    P.emit(nc)
    return nc, wtiles


dbg_tile = 0
dbg_layer = 0
DMAQ = "act"
N_ATT = 8
N_HG = 8
HG_NO_INTER = False
HG_NO_CHAIN = False


def _mat_of(inputs, l):
    w_ff2 = inputs["w_ff2"][l]
    m = {"w_in": inputs["w_in"][l], "w_o": inputs["w_o"][l], "w_ff1": inputs["w_ff1"][l],
         "w_ff2_0": w_ff2[0:4096], "w_ff2_1": w_ff2[4096:8192],
         "w_pg": inputs["w_ple_gate"][l], "w_pi": inputs["w_ple_in"][l]}
    for b in range(3):
        m["w_br%d" % b] = inputs["w_branch"][l, b]
    return m


def pack_weights(inputs, wtiles, n_layers=2):
    out = np.zeros((L, NL, 128, LW), np.float32)
    for l in range(n_layers):
        mats = _mat_of(inputs, l)
        flat = out[l].transpose(1, 0, 2).reshape(128, NL * LW)
        flat = np.zeros((128, NL * LW), np.float32)
        for (mat, kc, col, w, ld, off) in wtiles[l]:
            p = ld * LW + off
            flat[:, p:p + w] = mats[mat][kc * 128:(kc + 1) * 128, col:col + w]
        out[l] = flat.reshape(128, NL, LW).transpose(1, 0, 2)
    return out


def build_consts():
    c = np.zeros((128, NCST), np.float32)
    c[:, 0:128] = np.eye(128, dtype=np.float32)
    c[:, 128:256] = 1.0
    s = np.arange(128)[:, None]
    t = np.arange(128)[None, :]
    c[:, 256:384] = ((s // 64 == t // 64) & (s <= t)).astype(np.float32)
    r = np.ones(512, np.float32)
    r[::64] = 0.0
    c[:, 384:896] = r[None, :]
    return c


def build_params(inputs):
    p = np.zeros((128, NPRM), np.float32)
    for l in range(L):
        o = l * PRM_L
        p[:, o:o + 16] = inputs["g_mix"][l].reshape(16, 128).T
        p[:, o + 16:o + 32] = inputs["g_ff"][l].reshape(16, 128).T
        p[:, o + 32:o + 48] = inputs["g_ple"][l].reshape(16, 128).T
        cw = inputs["conv_w"][l]
        p[:, o + 48:o + 72] = cw.reshape(3, 8, 128).transpose(2, 1, 0).reshape(128, 24)
        p[:, o + 72:o + 80] = inputs["hg_lb_logits"][l].reshape(8, 128).T
        p[:, o + 80] = inputs["hg_norm_g"][l]
    p[:, 2 * PRM_L:2 * PRM_L + 16] = inputs["g_final"].reshape(16, 128).T
    return p


def build_bias_tables(inputs):
    tab = inputs["att_rel_bias"]
    k = np.arange(128)
    c2 = (k // 64)[:, None, None]
    b = (k % 64)[:, None, None]
    e = np.arange(10)[None, :, None]
    a = np.arange(64)[None, None, :]
    d = e - c2
    rel = d * 64 + a - b
    idx = np.clip(rel, -256, 256) + 256
    valid = (d >= 0) & (d <= 8)
    idx = np.broadcast_to(idx, (128, 10, 64))
    valid = np.broadcast_to(valid, (128, 10, 64))
    g = tab[:, :, idx]
    g = np.where(valid[None, None], g, np.float32(NEG)).astype(np.float32)
    return np.ascontiguousarray(g.reshape(L, 8, 128, 640))


_CACHE = {}


def _get_program(n_seq, n_tiles, n_layers, dbg):
    key = (n_seq, n_tiles, n_layers, dbg)
    if key not in _CACHE:
        _CACHE[key] = build_program(n_seq, n_tiles, n_layers, dbg)
    return _CACHE[key]


def kernel(x, p, w_in, conv_w, hg_lb_logits, hg_norm_g, att_rel_bias, w_branch, w_o,
           w_ff1, w_ff2, w_ple_in, w_ple_gate, g_mix, g_ff, g_ple, g_final):
    inputs = dict(x=x, p=p, w_in=w_in, conv_w=conv_w, hg_lb_logits=hg_lb_logits, hg_norm_g=hg_norm_g,
                  att_rel_bias=att_rel_bias, w_branch=w_branch, w_o=w_o, w_ff1=w_ff1, w_ff2=w_ff2,
                  w_ple_in=w_ple_in, w_ple_gate=w_ple_gate, g_mix=g_mix, g_ff=g_ff, g_ple=g_ple, g_final=g_final)
    inputs = {k: np.asarray(v, dtype=np.float32) for k, v in inputs.items()}
    B, S, _ = inputs["x"].shape
    n_cores = 8
    n_seq = B // n_cores
    n_tiles = S // T
    nc, wtiles = _get_program(n_seq, n_tiles, L, False)
    wpk = pack_weights(inputs, wtiles)
    prm = build_params(inputs)
    cst = build_consts()
    btd = build_bias_tables(inputs)
    in_maps = []
    for c in range(n_cores):
        xs = inputs["x"][c * n_seq:(c + 1) * n_seq]
        xT = np.ascontiguousarray(xs.reshape(n_seq, S, NCH, 128).transpose(0, 3, 2, 1))
        ps_ = inputs["p"][:, c * n_seq:(c + 1) * n_seq]
        pT = np.ascontiguousarray(ps_.reshape(L, n_seq, S, 2, 128).transpose(0, 1, 4, 3, 2))
        in_maps.append({"xT": xT, "pT": pT, "wpk": wpk, "prm": prm, "cst": cst, "btd": btd})
    res = run_bass_kernel_spmd(nc, in_maps, core_ids=list(range(n_cores)))
    out = np.empty((B, S, D), np.float32)
    for c in range(n_cores):
        oT = res.results[c]["outT"]
        out[c * n_seq:(c + 1) * n_seq] = oT.transpose(0, 3, 2, 1).reshape(n_seq, S, D)
    return out
```

```python
import numpy as np
from contextlib import ExitStack
import concourse.bass as bass
import concourse.mybir as mybir
from concourse.bass_utils import run_bass_kernel_spmd

F32 = mybir.dt.float32
BF16 = mybir.dt.bfloat16
AF = mybir.ActivationFunctionType
ALU = mybir.AluOpType

D = 2048
T = 512
NCH = 16
L = 2
EPS = 1e-6
LW = 4096
NSLOT = 4
NL = 160
PIECE = 8
NCAST = 8
COL = dict(cb=0, cc=1024, ch=2048, hq=3072, hf=4096, hi=5120, hg=6144, aq=7168, ak=8192,
           av=9216, ga=10240, gb=12288, gc=14336)
NEG = -30000.0
PRM_L = 81
NPRM = 2 * PRM_L + 16
NCST = 128 + 128 + 128 + 512


class Buf:
    __slots__ = ("name", "w", "r")

    def __init__(self, name=""):
        self.name = name
        self.w = None
        self.r = {}


class Prog:
    ENG = ("pe", "act", "dve", "pool", "sp")

    def __init__(self):
        self.streams = {e: [] for e in self.ENG}
        self.cnt = {}
        self.waited = {e: {} for e in self.ENG}
        self.last_dma = {}

    def _need(self, eng, tok, waits):
        if tok is None:
            return
        s, v = tok
        if eng == "pe" and s == "pe":
            return
        if self.waited[eng].get(s, 0) >= v:
            return
        if waits.get(s, 0) < v:
            waits[s] = v

    def op(self, eng, fn, reads=(), writes=(), sem=None, signal=True):
        waits = {}
        for b in reads:
            self._need(eng, b.w, waits)
        for b in writes:
            self._need(eng, b.w, waits)
            for s, v in b.r.items():
                self._need(eng, (s, v), waits)
        if sem is not None:
            self._need(eng, self.last_dma.get(sem), waits)
        for s, v in waits.items():
            self.waited[eng][s] = v
        if sem is None:
            s = eng
            inc = 1
        else:
            s = sem
            inc = 16
        tok = None
        if signal:
            self.cnt[s] = self.cnt.get(s, 0) + inc
            tok = (s, self.cnt[s])
            if sem is not None:
                self.last_dma[sem] = tok
            for b in reads:
                if b.r.get(s, 0) < tok[1]:
                    b.r[s] = tok[1]
            for b in writes:
                b.w = tok
                b.r = {}
        self.streams[eng].append((list(waits.items()), fn, s if signal else None, inc))
        return tok

    def group(self, eng, fns, reads=(), writes=()):
        n = len(fns)
        tok = None
        for i, fn in enumerate(fns):
            last = (i == n - 1)
            if i == 0 or last:
                tok = self.op(eng, fn, reads, writes, signal=last)
            else:
                self.streams[eng].append(([], fn, None, 1))
        return tok

    def wait_all(self, eng, bufs):
        self.op(eng, None, reads=bufs, signal=False)

    def emit(self, nc):
        names = set(self.cnt.keys())
        for e in self.ENG:
            for waits, fn, s, inc in self.streams[e]:
                for ws, wv in waits:
                    names.add(ws)
        with ExitStack() as st:
            sems = {n: st.enter_context(nc.semaphore("s_" + n)) for n in sorted(names)}
            block = st.enter_context(nc.Block())
            engmap = {"pe": block.tensor, "act": block.scalar, "dve": block.vector,
                      "pool": block.gpsimd, "sp": block.sync}
            for e in self.ENG:
                stream = self.streams[e]

                def body(engine, stream=stream):
                    for waits, fn, s, inc in stream:
                        for ws, wv in waits:
                            engine.wait_ge(sems[ws], wv)
                        if fn is None:
                            continue
                        ins = fn(engine)
                        if s is not None:
                            ins.then_inc(sems[s], inc)
                engmap[e](body)


def build_program(n_seq=2, n_tiles=4, n_layers=2, dbg=False):
    S = n_tiles * T
    nc = bass.Bass("TRN2", target_bir_lowering=False)
    P = Prog()
    xT = nc.dram_tensor("xT", [n_seq, 128, NCH, S], F32, kind="ExternalInput").ap()
    pT = nc.dram_tensor("pT", [L, n_seq, 128, 2, S], F32, kind="ExternalInput").ap()
    wpk = nc.dram_tensor("wpk", [L, NL, 128, LW], F32, kind="ExternalInput").ap()
    prm = nc.dram_tensor("prm", [128, NPRM], F32, kind="ExternalInput").ap()
    cst = nc.dram_tensor("cst", [128, NCST], F32, kind="ExternalInput").ap()
    btd = nc.dram_tensor("btd", [L, 8, 128, 640], F32, kind="ExternalInput").ap()
    outT = nc.dram_tensor("outT", [n_seq, 128, NCH, S], F32, kind="ExternalOutput").ap()
    wbf = [nc.dram_tensor("wbf%d" % l, [NL, 128, LW], BF16).ap() for l in range(L)]
    kcd = nc.dram_tensor("kcache", [L, 128, 8, T], BF16).ap()
    vcd = nc.dram_tensor("vcache", [L, 128, 4, 1024], BF16).ap()
    dbg_outs = []

    def sb(name, shape, dt=F32):
        return nc.alloc_sbuf_tensor(name, shape, dt)

    h = sb("h", [128, NCH, T]); hb = [Buf("h%d" % c) for c in range(NCH)]
    xn = sb("xn", [128, NCH, T], BF16); xb = [Buf("xn%d" % c) for c in range(NCH)]
    ring = [sb("ring%d" % i, [128, LW], BF16) for i in range(NSLOT)]
    ringb = [Buf("ring%d" % i) for i in range(NSLOT)]
    ybig = sb("ybig", [128, 24, T], BF16); yb = [Buf("y%d" % i) for i in range(24)]
    r2 = sb("r2", [128, 32 * T], BF16); r2b = [Buf("r2_%d" % i) for i in range(32)]
    KT = r2[:, 0:16 * T].rearrange("p (h k) -> p h k", h=8)
    Vt = r2[:, 16 * T:32 * T].rearrange("p (b f) -> p b f", b=8)
    merged = r2[:, 0:16 * T].rearrange("p (c t) -> p c t", c=16)
    abuf = r2[:, :].rearrange("p (c t) -> p c t", c=32)
    ostage = r2[:, :].bitcast(F32).rearrange("p (c t) -> p c t", c=16)

    def ktb(hh, half):
        return r2b[hh * 2 + half]

    def vtb(kb):
        return [r2b[16 + 2 * kb], r2b[17 + 2 * kb]]
    prm_s = sb("prm_s", [128, NPRM]); prmb = Buf("prm")
    cst_s = sb("cst_s", [128, NCST]); cstb = Buf("cst")
    ident = sb("ident", [128, 128], BF16)
    ones = sb("ones", [128, 128], BF16)
    mask2 = cst_s[:, 256:384]
    resetm = cst_s[:, 384:896]
    lb_t = sb("lb_t", [128, L, 8]); oml_t = sb("oml_t", [128, L, 8])
    rstd = sb("rstd", [128, T]); rstdb = Buf("rstd")
    S_st = [sb("S%d" % l, [128, 8, 128]) for l in range(L)]
    Sx = sb("Sx", [128, 3, 128]); Sxb = [Buf() for _ in range(3)]
    Sb = [[Buf("S%d_%d" % (l, hh)) for hh in range(8)] for l in range(L)]
    ctail = [sb("ctail%d" % l, [128, 8, 2]) for l in range(L)]
    ctb = [[Buf() for j in range(8)] for l in range(L)]
    c_t1 = sb("c_t1", [128, T]); c_t1b = Buf()
    c_u = sb("c_u", [128, T + 2]); c_ub = Buf()
    c_acc = sb("c_acc", [128, T]); c_accb = Buf()
    qT = [sb("qT%d" % i, [128, T], BF16) for i in range(2)]; qTb = [Buf(), Buf()]
    BTs = [sb("BT%d" % i, [128, 640], BF16) for i in range(2)]; BTb = [Buf(), Buf()]
    a_pt = [sb("a_pt%d" % i, [128, T], BF16) for i in range(4)]; a_ptb = [Buf() for _ in range(4)]
    a_rd = sb("a_rd", [128, T]); a_rdb = Buf()
    NT_ = 6
    ht = [sb("ht%d" % i, [128, T]) for i in range(NT_)]; htb = [Buf("ht%d" % i) for i in range(NT_)]
    h_q = sb("h_q", [128, T], BF16); h_qb = Buf()
    h_k = sb("h_k", [128, T], BF16); h_kb = Buf()
    h_qd = sb("h_qd", [128, T]); h_qdb = Buf()
    h_kh = sb("h_kh", [128, T], BF16); h_khb = Buf()
    h_khT = sb("h_khT", [128, 4, 128], BF16); h_khTb = [Buf() for _ in range(4)]
    h_at = [sb("h_at%d" % i, [128, 128], BF16) for i in range(4)]; h_atb = [Buf() for _ in range(4)]
    h_dec = sb("h_dec", [128, 8]); h_decb = Buf()
    h_gs = sb("h_gs", [128, T]); h_gsb = Buf()
    h_sq = sb("h_sq", [128, T], BF16); h_sqb = Buf()
    m_e = [sb("m_e%d" % i, [128, T]) for i in range(2)]; m_eb = [Buf(), Buf()]
    m_acc = sb("m_acc", [128, T]); m_accb = Buf()
    f_r = [sb("f_r%d" % i, [128, T]) for i in range(2)]; f_rb = [Buf(), Buf()]
    pbf = sb("pbf", [128, 2, T], BF16); pbfb = Buf()

    ps = [nc.alloc_psum_tensor("ps%d" % i, [128, T], F32) for i in range(8)]
    psb = [Buf("ps%d" % i) for i in range(8)]
    rot = [0]

    nrot = [6]

    def nbank():
        i = rot[0] % nrot[0]
        rot[0] = (i + 1) % nrot[0]
        return i
    rot2 = [0]

    def nsmall():
        i = 4 + rot2[0]
        rot2[0] = (rot2[0] + 1) % 2
        return i
    qsl = [(4, 0), (5, 0)]
    qslb = [psb[4], psb[5]]
    rotq = [0]

    def nquarter():
        i = rotq[0]
        rotq[0] = (rotq[0] + 1) % 2
        return i

    P.op("sp", lambda e: e.dma_start(out=prm_s[:], in_=prm[:, :]), writes=[prmb], sem="ldp")
    P.op("sp", lambda e: e.dma_start(out=cst_s[:], in_=cst[:, :]), writes=[cstb], sem="ldc")
    identb = Buf(); onesb = Buf(); lbb = Buf()
    P.op("dve", lambda e: e.tensor_copy(out=ident[:], in_=cst_s[:, 0:128]), reads=[cstb], writes=[identb])
    P.op("dve", lambda e: e.tensor_copy(out=ones[:], in_=cst_s[:, 128:256]), reads=[cstb], writes=[onesb])
    P.op("dve", lambda e: e.memset(lb_t[:, 0, :], 0.0), writes=[lbb])
    P.op("dve", lambda e: e.memset(oml_t[:, 0, :], 1.0), reads=[lbb], writes=[lbb])
    l0 = prm_s[:, 72:80]
    l1 = prm_s[:, PRM_L + 72:PRM_L + 80]
    P.op("dve", lambda e: e.tensor_tensor(out=lb_t[:, 1, :], in0=l0, in1=l1, op=ALU.subtract), reads=[prmb, lbb], writes=[lbb])
    P.op("act", lambda e: e.activation(out=oml_t[:, 1, :], in_=lb_t[:, 1, :], func=AF.Exp), reads=[lbb], writes=[lbb])
    P.op("dve", lambda e: e.tensor_scalar(out=lb_t[:, 1, :], in0=oml_t[:, 1, :], scalar1=1.0, scalar2=None, op0=ALU.add), reads=[lbb], writes=[lbb])
    P.op("dve", lambda e: e.reciprocal(out=lb_t[:, 1, :], in_=lb_t[:, 1, :]), reads=[lbb], writes=[lbb])
    P.op("dve", lambda e: e.tensor_tensor(out=oml_t[:, 1, :], in0=oml_t[:, 1, :], in1=lb_t[:, 1, :], op=ALU.mult), reads=[lbb], writes=[lbb])

    wbfb = [[Buf("wbf%d_%d" % (l, i)) for i in range(NL)] for l in range(L)]
    gload = [0]
    wtiles = {}

    class WS:
        def __init__(self, l, first_use):
            self.l = l
            self.first_use = first_use
            self.pos = 0
            self.cur = -1
            self.tiles = []
            self.cur_slot = None

        def tile(self, mat, kc, col, w):
            off = self.pos % LW
            if off + w > LW:
                self.pos += LW - off
            ld = self.pos // LW
            off = self.pos % LW
            self.tiles.append((mat, kc, col, w, ld, off))
            self.pos += w
            if ld > self.cur:
                assert ld == self.cur + 1 and ld < NL
                slot = gload[0] % NSLOT
                gload[0] += 1
                l = self.l
                if self.first_use:
                    P.op("pool", lambda e, slot=slot, l=l, ld=ld: e.dma_start(out=ring[slot][:], in_=wpk[l, ld]),
                         writes=[ringb[slot]], sem="wq%d" % slot)
                    P.op("sp", lambda e, slot=slot, l=l, ld=ld: e.dma_start(out=wbf[l][ld], in_=ring[slot][:]),
                         reads=[ringb[slot]], writes=[wbfb[l][ld]], sem="ws%d" % slot)
                else:
                    P.op("sp", lambda e, slot=slot, l=l, ld=ld: e.dma_start(out=ring[slot][:], in_=wbf[l][ld]),
                         reads=[wbfb[l][ld]], writes=[ringb[slot]], sem="w%d" % slot)
                self.cur = ld
                self.cur_slot = slot
            return ring[self.cur_slot][:, off:off + w], ringb[self.cur_slot]

    def proj_group(ws, mat, col, nk, rhs_of, bank, extra_reads):
        pairs = []
        rb = set()
        for kc in range(nk):
            wt, wb = ws.tile(mat, kc, col, 128)
            pairs.append((wt, rhs_of(kc)))
            rb.add(wb)
        n = len(pairs)
        fns = [(lambda e, a=a, b=b, i=i: e.matmul(ps[bank][:, :], lhsT=a, rhs=b, start=(i == 0), stop=(i == n - 1)))
               for i, (a, b) in enumerate(pairs)]
        return P.group("pe", fns, reads=list(rb) + list(extra_reads), writes=[psb[bank]])

    def proj_pieces(ws, mat, col, nk, rhs_of, rhs_bufs_of, bank, piece=4):
        for p0 in range(0, nk, piece):
            pairs = []
            rb = []
            for kc in range(p0, min(nk, p0 + piece)):
                wt, wb = ws.tile(mat, kc, col, 128)
                pairs.append((wt, rhs_of(kc), kc))
                rb.append(wb)
                rb.extend(rhs_bufs_of(kc))
            fns = [(lambda e, a=a, b=b, kc=kc: e.matmul(ps[bank][:, :], lhsT=a, rhs=b, start=(kc == 0), stop=(kc == nk - 1)))
                   for (a, b, kc) in pairs]
            P.group("pe", fns, reads=rb, writes=[psb[bank]])
            yield

    def proj_multi(ws, specs):
        banks = [nbank() for _ in specs]
        for kc in range(NCH):
            fns = []
            rb = [xb[kc]]
            for (mat, col), bank in zip(specs, banks):
                wt, wb = ws.tile(mat, kc, col, 128)
                rb.append(wb)
                fns.append(lambda e, wt=wt, bank=bank, kc=kc: e.matmul(ps[bank][:, :], lhsT=wt, rhs=xn[:, kc, :], start=(kc == 0), stop=(kc == NCH - 1)))
            P.group("pe", fns, reads=rb, writes=[psb[b] for b in banks])
        return banks

    def vproj(ws, col0, dst_of, dst_bufs_of):
        for half in range(2):
            fns = []
            rb = set()
            for kc in range(NCH):
                wt, wb = ws.tile("w_in", kc, col0 + half * 512, 512)
                rb.add(wb)
                for tb in range(4):
                    fns.append(lambda e, wt=wt, kc=kc, tb=tb: e.matmul(
                        ps[tb][:, :], lhsT=xn[:, kc, tb * 128:(tb + 1) * 128], rhs=wt,
                        start=(kc == 0), stop=(kc == NCH - 1)))
            P.group("pe", fns, reads=list(rb) + xb, writes=psb[0:4])
            for tb in range(4):
                eng = "act" if tb % 2 == 0 else "dve"
                dst = dst_of(tb)[:, half * 512:(half + 1) * 512]
                if eng == "act":
                    P.op("act", lambda e, dst=dst, tb=tb: e.activation(out=dst, in_=ps[tb][:, :], func=AF.Copy),
                         reads=[psb[tb]], writes=[dst_bufs_of(tb)[half]])
                else:
                    P.op("dve", lambda e, dst=dst, tb=tb: e.tensor_copy(out=dst, in_=ps[tb][:, :]),
                         reads=[psb[tb]], writes=[dst_bufs_of(tb)[half]])
        rot[0] = 0

    def rmsnorm(gcol, dst, dst_bufs, dst_is_xn=True):
        for q in range(8):
            sl = slice(2 * q, 2 * q + 2)
            if q % 2 == 0:
                P.op("act", lambda e, sl=sl: e.activation(out=xn[:, sl, :], in_=h[:, sl, :], func=AF.Square, scale=float(D ** -0.5)),
                     reads=hb[sl], writes=xb[sl])
            else:
                P.op("dve", lambda e, sl=sl: e.scalar_tensor_tensor(out=xn[:, sl, :], in0=h[:, sl, :], scalar=float(1.0 / D),
                                                                    in1=h[:, sl, :], op0=ALU.mult, op1=ALU.mult),
                     reads=hb[sl], writes=xb[sl])
        for q in range(8):
            fns = [(lambda e, c=c: e.matmul(ps[7][:, :], lhsT=ones[:, :], rhs=xn[:, c, :], start=(c == 0), stop=(c == NCH - 1)))
                   for c in range(2 * q, 2 * q + 2)]
            P.group("pe", fns, reads=xb[2 * q:2 * q + 2] + [onesb], writes=[psb[7]])
        P.op("act", lambda e: e.activation(out=rstd[:, :], in_=ps[7][:, :], func=AF.Ln, bias=EPS, scale=1.0),
             reads=[psb[7]], writes=[rstdb])
        P.op("act", lambda e: e.activation(out=rstd[:, :], in_=rstd[:, :], func=AF.Exp, scale=-0.5),
             reads=[rstdb], writes=[rstdb])
        for c in range(NCH):
            P.op("dve", lambda e, c=c: e.scalar_tensor_tensor(out=dst[:, c, :], in0=h[:, c, :],
                                                              scalar=prm_s[:, gcol + c:gcol + c + 1], in1=rstd[:, :],
                                                              op0=ALU.mult, op1=ALU.mult),
                 reads=[hb[c], rstdb, prmb], writes=dst_bufs(c))

    def dump(name, ap, bufs, shape):
        if not dbg:
            return
        o = nc.dram_tensor(name, shape, F32, kind="ExternalOutput").ap()
        ob = Buf()
        P.op("pool", lambda e: e.dma_start(out=o, in_=ap), reads=bufs, writes=[ob], sem="dbg_" + name)
        dbg_outs.append(ob)

    outb = Buf("out")
    xldb = Buf()
    kcb = [Buf() for _ in range(L)]
    vcb = [Buf() for _ in range(L)]

    for sq in range(n_seq):
        for ti in range(n_tiles):
            t0 = ti * T
            first = (ti == 0)
            P.op(DMAQ, lambda e, sq=sq, t0=t0: e.dma_start(out=h[:, :, :], in_=xT[sq, :, :, t0:t0 + T]),
                 writes=hb, sem="ldx")
            for l in range(n_layers):
                ws = WS(l, sq == 0 and ti == 0)
                pr = l * PRM_L
                rmsnorm(pr + 0, xn, lambda c: [xb[c]])
                if not first:
                    P.op(DMAQ, lambda e, l=l: e.dma_start(out=KT[:, :, 0:T], in_=kcd[l]),
                         reads=[kcb[l]], writes=[ktb(hh, 0) for hh in range(8)], sem="ldk")
                    P.op(DMAQ, lambda e, l=l: e.dma_start(out=Vt[:, 0:4, :], in_=vcd[l]),
                         reads=[vcb[l]], writes=[b for kb in range(4) for b in vtb(kb)], sem="ldv")
                for j in range(8):
                    bk = {}
                    if j == 0:
                        mb = proj_multi(ws, [("w_in", COL[part]) for part in ("cb", "cc", "ch")])
                        bk = dict(zip(("cb", "cc", "ch"), mb))
                    else:
                        for part in ("cb", "cc", "ch"):
                            bk[part] = nbank()
                            proj_group(ws, "w_in", COL[part] + j * 128, NCH, lambda kc: xn[:, kc, :], bk[part], xb)
                    P.op("act", lambda e, b=bk["cc"]: e.activation(out=c_t1[:, :], in_=ps[b][:, :], func=AF.Copy),
                         reads=[psb[bk["cc"]]], writes=[c_t1b])
                    if first:
                        P.op("dve", lambda e: e.memset(c_u[:, 0:2], 0.0), writes=[c_ub])
                    else:
                        P.op("act", lambda e, l=l, j=j: e.activation(out=c_u[:, 0:2], in_=ctail[l][:, j, :], func=AF.Copy),
                             reads=[ctb[l][j]], writes=[c_ub])
                    P.op("dve", lambda e, b=bk["ch"]: e.tensor_tensor(out=c_u[:, 2:T + 2], in0=ps[b][:, :], in1=c_t1[:, :], op=ALU.mult),
                         reads=[psb[bk["ch"]], c_t1b, c_ub], writes=[c_ub])
                    P.op("act", lambda e, l=l, j=j: e.activation(out=ctail[l][:, j, :], in_=c_u[:, T:T + 2], func=AF.Copy),
                         reads=[c_ub], writes=[ctb[l][j]])
                    cw = pr + 48 + j * 3
                    P.op("dve", lambda e, cw=cw: e.tensor_scalar(out=c_acc[:, :], in0=c_u[:, 0:T], scalar1=prm_s[:, cw:cw + 1], scalar2=None, op0=ALU.mult),
                         reads=[c_ub, prmb], writes=[c_accb])
                    P.op("dve", lambda e, cw=cw: e.scalar_tensor_tensor(out=c_acc[:, :], in0=c_u[:, 1:T + 1], scalar=prm_s[:, cw + 1:cw + 2], in1=c_acc[:, :], op0=ALU.mult, op1=ALU.add),
                         reads=[c_ub, c_accb], writes=[c_accb])
                    P.op("dve", lambda e, cw=cw: e.scalar_tensor_tensor(out=c_acc[:, :], in0=c_u[:, 2:T + 2], scalar=prm_s[:, cw + 2:cw + 3], in1=c_acc[:, :], op0=ALU.mult, op1=ALU.add),
                         reads=[c_ub, c_accb], writes=[c_accb])
                    P.op("dve", lambda e, b=bk["cb"], j=j: e.tensor_tensor(out=ybig[:, j, :], in0=ps[b][:, :], in1=c_acc[:, :], op=ALU.mult),
                         reads=[psb[bk["cb"]], c_accb], writes=[yb[j]])
                nrot[0] = 4
                vproj(ws, COL["av"], lambda tb: Vt[:, 4 + tb, :], lambda tb: vtb(4 + tb))
                nrot[0] = 3
                sbanks = (3, 4, 5)
                DEPTH = 3
                def att_proj(hh):
                    qi = hh % 2
                    P.op("pool", lambda e, l=l, hh=hh, qi=qi: e.dma_start(out=BTs[qi][:, :], in_=btd[l, hh]),
                         writes=[BTb[qi]], sem="ldb%d" % qi)
                    bq = nbank()
                    yield from proj_pieces(ws, "w_in", COL["aq"] + hh * 128, NCH, lambda kc: xn[:, kc, :], lambda kc: [xb[kc]], bq)
                    P.op("act", lambda e, bq=bq, qi=qi: e.activation(out=qT[qi][:, :], in_=ps[bq][:, :], func=AF.Copy, scale=float(128 ** -0.5)),
                         reads=[psb[bq]], writes=[qTb[qi]])
                    bkk = nbank()
                    yield from proj_pieces(ws, "w_in", COL["ak"] + hh * 128, NCH, lambda kc: xn[:, kc, :], lambda kc: [xb[kc]], bkk)
                    P.op("dve", lambda e, bkk=bkk, hh=hh: e.tensor_copy(out=KT[:, hh, T:2 * T], in_=ps[bkk][:, :]),
                         reads=[psb[bkk]], writes=[ktb(hh, 1)])

                def att_chain(hh, filler):
                    qi = hh % 2
                    kbs = [4, 5, 6, 7] if first else [3, 4, 0, 1, 2, 5, 6, 7]
                    nk = len(kbs)
                    geo = []
                    for kb in kbs:
                        lo = max(0, 2 * kb - 8)
                        hi = min(7, 2 * kb + 1)
                        geo.append((kb, (hi - lo + 1) * 64, lo * 64, (lo + 8 - 2 * kb) * 64))

                    def att_front(ii):
                        kb, n, c0, e0 = geo[ii]
                        sbk = sbanks[ii % 3]
                        pi = ii % 4
                        kt_ap = KT[:, hh, kb * 128:(kb + 1) * 128]
                        q_ap = qT[qi][:, c0:c0 + n]
                        bt_ap = BTs[qi][:, e0:e0 + n]
                        P.group("pe", [
                            lambda e, sbk=sbk, n=n, kt_ap=kt_ap, q_ap=q_ap: e.matmul(ps[sbk][:, 0:n], lhsT=kt_ap, rhs=q_ap, start=True, stop=False),
                            lambda e, sbk=sbk, n=n, bt_ap=bt_ap: e.matmul(ps[sbk][:, 0:n], lhsT=ident[:, :], rhs=bt_ap, start=False, stop=True)],
                            reads=[ktb(hh, kb // 4), qTb[qi], BTb[qi], identb], writes=[psb[sbk]])
                        P.op("act", lambda e, sbk=sbk, pi=pi, n=n: e.activation(out=a_pt[pi][:, 0:n], in_=ps[sbk][:, 0:n], func=AF.Exp),
                             reads=[psb[sbk]], writes=[a_ptb[pi]])

                    def att_back(ii):
                        kb, n, c0, e0 = geo[ii]
                        pi = ii % 4
                        st_ = (ii == 0)
                        sp_ = (ii == nk - 1)
                        v_ap = Vt[:, kb, hh * 128:(hh + 1) * 128]
                        P.op("pe", lambda e, pi=pi, c0=c0, n=n, st_=st_, sp_=sp_: e.matmul(ps[7][:, c0:c0 + n], lhsT=ones[:, :], rhs=a_pt[pi][:, 0:n], start=st_, stop=sp_),
                             reads=[a_ptb[pi], onesb], writes=[psb[7]])
                        P.op("pe", lambda e, pi=pi, c0=c0, n=n, st_=st_, sp_=sp_, v_ap=v_ap: e.matmul(ps[6][:, c0:c0 + n], lhsT=v_ap, rhs=a_pt[pi][:, 0:n], start=st_, stop=sp_),
                             reads=[a_ptb[pi]] + vtb(kb), writes=[psb[6]])
                    for ii in range(nk):
                        att_front(ii)
                        if ii >= DEPTH - 1:
                            att_back(ii - (DEPTH - 1))
                        next(filler, None)
                    for ii in range(max(0, nk - (DEPTH - 1)), nk):
                        att_back(ii)
                        next(filler, None)
                    for _ in filler:
                        pass
                    P.op("dve", lambda e: e.reciprocal(out=a_rd[:, :], in_=ps[7][:, :]), reads=[psb[7]], writes=[a_rdb])
                    P.op("dve", lambda e, hh=hh: e.tensor_tensor(out=ybig[:, 16 + hh, :], in0=ps[6][:, :], in1=a_rd[:, :], op=ALU.mult),
                         reads=[psb[6], a_rdb], writes=[yb[16 + hh]])
                if N_ATT > 0:
                    for _ in att_proj(0):
                        pass
                for hh in range(N_ATT):
                    att_chain(hh, att_proj(hh + 1) if hh + 1 < N_ATT else iter(()))
                nrot[0] = 4
                if ti < n_tiles - 1:
                    P.op(DMAQ, lambda e, l=l: e.dma_start(out=kcd[l], in_=KT[:, :, T:2 * T]),
                         reads=[ktb(hh, 1) for hh in range(8)], writes=[kcb[l]], sem="stk")
                    P.op(DMAQ, lambda e, l=l: e.dma_start(out=vcd[l], in_=Vt[:, 4:8, :]),
                         reads=[b for kb in range(4, 8) for b in vtb(kb)], writes=[vcb[l]], sem="stv")
                vproj(ws, COL["hi"], lambda tb: Vt[:, tb, :], lambda tb: vtb(tb))
                if first:
                    P.op("dve", lambda e, l=l: e.memset(S_st[l][:, :, :], 0.0), writes=Sb[l])
                def hg_proj(hh):
                    bq = nbank(); proj_group(ws, "w_in", COL["hq"] + hh * 128, NCH, lambda kc: xn[:, kc, :], bq, xb)
                    bf = nbank(); proj_group(ws, "w_in", COL["hf"] + hh * 128, NCH, lambda kc: xn[:, kc, :], bf, xb)
                    bg = nbank(); proj_group(ws, "w_in", COL["hg"] + hh * 128, NCH, lambda kc: xn[:, kc, :], bg, xb)
                    return bq, bf, bg

                def hg_prep(hh, banks):
                    bq, bf, bg = banks
                    lbc = lb_t[:, l, hh:hh + 1]
                    omc = oml_t[:, l, hh:hh + 1]
                    P.op("act", lambda e: e.activation(out=ht[0][:, :], in_=ps[bf][:, :], func=AF.Exp, scale=-1.0),
                         reads=[psb[bf]], writes=[htb[0]])
                    P.op("act", lambda e: e.activation(out=h_gs[:, :], in_=ps[bg][:, :], func=AF.Copy), reads=[psb[bg]], writes=[h_gsb])
                    P.op("act", lambda e: e.activation(out=h_qd[:, :], in_=ps[bq][:, :], func=AF.Copy), reads=[psb[bq]], writes=[h_qdb])
                    P.op("act", lambda e: e.activation(out=ht[1][:, :], in_=ht[0][:, :], func=AF.Ln, bias=1.0, scale=1.0), reads=[htb[0]], writes=[htb[1]])
                    P.op("act", lambda e: e.activation(out=ht[1][:, :], in_=ht[1][:, :], func=AF.Exp, scale=-1.0), reads=[htb[1]], writes=[htb[1]])
                    P.op("dve", lambda e: e.tensor_tensor(out=ht[2][:, :], in0=ht[0][:, :], in1=ht[1][:, :], op=ALU.mult),
                         reads=[htb[0], htb[1]], writes=[htb[2]])
                    P.op("act", lambda e: e.activation(out=ht[3][:, :], in_=ht[1][:, :], func=AF.Ln, bias=lbc, scale=omc),
                         reads=[htb[1], lbb], writes=[htb[3]])
                    P.op("dve", lambda e: e.tensor_tensor_scan(out=ht[4][:, :], data0=resetm, data1=ht[3][:, :], initial=0.0, op0=ALU.mult, op1=ALU.add),
                         reads=[htb[3], cstb], writes=[htb[4]])
                    G3 = ht[4][:, :].rearrange("p (c t) -> p c t", t=64)
                    Gm3 = ht[5][:, :].rearrange("p (c t) -> p c t", t=64)
                    P.op("dve", lambda e: e.tensor_tensor(out=Gm3, in0=G3, in1=G3[:, :, 31:32].broadcast_to([128, 8, 64]), op=ALU.subtract),
                         reads=[htb[4]], writes=[htb[5]])
                    D3 = ht[3][:, :].rearrange("p (c t) -> p c t", t=64)
                    P.op("dve", lambda e: e.tensor_tensor(out=D3, in0=Gm3[:, :, 63:64].broadcast_to([128, 8, 64]), in1=Gm3, op=ALU.subtract),
                         reads=[htb[5]], writes=[htb[3]])
                    P.op("act", lambda e: e.activation(out=ht[3][:, :], in_=ht[3][:, :], func=AF.Exp), reads=[htb[3]], writes=[htb[3]])
                    P.op("dve", lambda e: e.scalar_tensor_tensor(out=h_kh[:, :], in0=ht[2][:, :], scalar=omc, in1=ht[3][:, :], op0=ALU.mult, op1=ALU.mult),
                         reads=[htb[2], htb[3], lbb], writes=[h_khb])
                    P.op("act", lambda e: e.activation(out=ht[1][:, :], in_=ht[5][:, :], func=AF.Exp, scale=-1.0), reads=[htb[5]], writes=[htb[1]])
                    P.op("dve", lambda e: e.scalar_tensor_tensor(out=h_k[:, :], in0=ht[2][:, :], scalar=omc, in1=ht[1][:, :], op0=ALU.mult, op1=ALU.mult),
                         reads=[htb[2], htb[1], lbb], writes=[h_kb])
                    P.op("act", lambda e: e.activation(out=ht[0][:, :], in_=ht[5][:, :], func=AF.Exp), reads=[htb[5]], writes=[htb[0]])
                    P.op("dve", lambda e: e.tensor_tensor(out=h_q[:, :], in0=h_qd[:, :], in1=ht[0][:, :], op=ALU.mult),
                         reads=[h_qdb, htb[0]], writes=[h_qb])
                    P.op("act", lambda e: e.activation(out=h_dec[:, :], in_=G3[:, :, 63], func=AF.Exp),
                         reads=[htb[4]], writes=[h_decb])
                    P.op("act", lambda e: e.activation(out=ht[0][:, :], in_=ht[4][:, :], func=AF.Exp), reads=[htb[4]], writes=[htb[0]])
                    P.op("dve", lambda e: e.tensor_tensor(out=h_qd[:, :], in0=h_qd[:, :], in1=ht[0][:, :], op=ALU.mult),
                         reads=[h_qdb, htb[0]], writes=[h_qdb])
                    P.op("act", lambda e: e.activation(out=ht[1][:, :], in_=h_gs[:, :], func=AF.Exp, scale=-1.0), reads=[h_gsb], writes=[htb[1]])
                    P.op("act", lambda e: e.activation(out=ht[1][:, :], in_=ht[1][:, :], func=AF.Ln, bias=1.0, scale=1.0), reads=[htb[1]], writes=[htb[1]])
                    P.op("act", lambda e: e.activation(out=ht[1][:, :], in_=ht[1][:, :], func=AF.Exp, scale=-1.0), reads=[htb[1]], writes=[htb[1]])
                    P.op("dve", lambda e: e.tensor_tensor(out=h_gs[:, :], in0=h_gs[:, :], in1=ht[1][:, :], op=ALU.mult),
                         reads=[h_gsb, htb[1]], writes=[h_gsb])

                def hg_smalls(hh):
                    for tb in range(4):
                        tsl = slice(tb * 128, (tb + 1) * 128)
                        bk = nsmall()
                        P.op("pe", lambda e, bk=bk, tsl=tsl: e.matmul(ps[bk][:, 0:128], lhsT=h_kh[:, tsl], rhs=ident[:, :], start=True, stop=True),
                             reads=[h_khb, identb], writes=[psb[bk]])
                        P.op("act", lambda e, bk=bk, tb=tb: e.activation(out=h_khT[:, tb, :], in_=ps[bk][:, 0:128], func=AF.Copy),
                             reads=[psb[bk]], writes=[h_khTb[tb]])
                    for tb in range(4):
                        tsl = slice(tb * 128, (tb + 1) * 128)
                        bk = nsmall()
                        P.op("pe", lambda e, bk=bk, tsl=tsl: e.matmul(ps[bk][:, 0:128], lhsT=h_k[:, tsl], rhs=h_q[:, tsl], start=True, stop=True),
                             reads=[h_kb, h_qb], writes=[psb[bk]])
                        P.op("dve", lambda e, bk=bk, tb=tb: e.tensor_tensor(out=h_at[tb][:, :], in0=ps[bk][:, 0:128], in1=mask2, op=ALU.mult),
                             reads=[psb[bk], cstb], writes=[h_atb[tb]])
                    n_ds = 7 if ti == n_tiles - 1 else 8
                    for c in range(n_ds):
                        bank = 4 + c % 2
                        off = (c // 2) * 128
                        tb = c // 2
                        rs = slice((c % 2) * 64, (c % 2) * 64 + 64)
                        v_ap = Vt[rs, tb, hh * 128:(hh + 1) * 128]
                        P.op("pe", lambda e, bank=bank, off=off, rs=rs, tb=tb, v_ap=v_ap: e.matmul(
                            ps[bank][:, off:off + 128], lhsT=h_khT[rs, tb, :], rhs=v_ap, start=True, stop=True),
                            reads=[h_khTb[tb]] + vtb(tb), writes=[psb[bank]])

                    def stbuf(c):
                        if c == 0:
                            return S_st[l][:, hh, :], Sb[l][hh]
                        return Sx[:, (c - 1) % 3, :], Sxb[(c - 1) % 3]
                    for tb in range(4):
                        tsl = slice(tb * 128, (tb + 1) * 128)
                        v_ap = Vt[:, tb, hh * 128:(hh + 1) * 128]
                        P.op("pe", lambda e, tb=tb, tsl=tsl, v_ap=v_ap: e.matmul(ps[6][:, tsl], lhsT=v_ap, rhs=h_at[tb][:, :], start=True, stop=False),
                             reads=[h_atb[tb]] + vtb(tb), writes=[psb[6]])
                        for cc in range(2):
                            c = 2 * tb + cc
                            csl = slice(c * 64, (c + 1) * 64)
                            src, srcb = stbuf(c)
                            if not HG_NO_INTER:
                                P.op("pe", lambda e, csl=csl, src=src, cc=cc: e.matmul(ps[6][:, csl], lhsT=src, rhs=h_qd[:, csl], start=False, stop=(cc == 1)),
                                     reads=[srcb, h_qdb], writes=[psb[6]])
                            if c < n_ds and not HG_NO_CHAIN:
                                bank = 4 + c % 2
                                off = (c // 2) * 128
                                if c == 7:
                                    dst, dstb = S_st[l][:, hh, :], Sb[l][hh]
                                else:
                                    dst, dstb = stbuf(c + 1)
                                P.op("dve", lambda e, bank=bank, off=off, src=src, dst=dst, c=c: e.scalar_tensor_tensor(
                                    out=dst, in0=src, scalar=h_dec[:, c:c + 1], in1=ps[bank][:, off:off + 128], op0=ALU.mult, op1=ALU.add),
                                    reads=[srcb, h_decb, psb[bank]], writes=[dstb])
                    P.op("act", lambda e: e.activation(out=h_sq[:, :], in_=ps[6][:, :], func=AF.Square, scale=float(128 ** -0.5)),
                         reads=[psb[6]], writes=[h_sqb])
                    P.op("pe", lambda e: e.matmul(ps[7][:, :], lhsT=ones[:, :], rhs=h_sq[:, :], start=True, stop=True),
                         reads=[h_sqb, onesb], writes=[psb[7]])
                    P.op("act", lambda e: e.activation(out=ht[0][:, :], in_=ps[7][:, :], func=AF.Ln, bias=EPS, scale=1.0), reads=[psb[7]], writes=[htb[0]])
                    P.op("act", lambda e: e.activation(out=ht[0][:, :], in_=ht[0][:, :], func=AF.Exp, scale=-0.5), reads=[htb[0]], writes=[htb[0]])
                    P.op("dve", lambda e: e.tensor_tensor(out=ht[1][:, :], in0=ps[6][:, :], in1=ht[0][:, :], op=ALU.mult),
                         reads=[psb[6], htb[0]], writes=[htb[1]])
                    ngc = prm_s[:, pr + 80:pr + 81]
                    P.op("dve", lambda e, hh=hh, ngc=ngc: e.scalar_tensor_tensor(out=ybig[:, 8 + hh, :], in0=ht[1][:, :], scalar=ngc, in1=h_gs[:, :], op0=ALU.mult, op1=ALU.mult),
                         reads=[htb[1], h_gsb, prmb], writes=[yb[8 + hh]])
                nrot[0] = 4
                if N_HG > 0:
                    hg_banks = hg_proj(0)
                for hh in range(N_HG):
                    hg_prep(hh, hg_banks)
                    if hh + 1 < N_HG:
                        hg_banks = hg_proj(hh + 1)
                    hg_smalls(hh)
                if dbg and sq == 0 and ti == dbg_tile and l == dbg_layer:
                    for nm, a_, b_ in (("d_yconv", 0, 8), ("d_yhg", 8, 16), ("d_yatt", 16, 24)):
                        dump(nm, ybig[:, a_:b_, :], yb[a_:b_], [128, 8, T])
                nrot[0] = 6
                for j in range(NCH):
                    for br, gname in enumerate(("ga", "gb", "gc")):
                        bgt = nbank()
                        proj_group(ws, "w_in", COL[gname] + j * 128, NCH, lambda kc: xn[:, kc, :], bgt, xb)
                        bro = nbank()
                        yoff = (0, 8, 16)[br]
                        proj_group(ws, "w_br%d" % br, j * 128, 8, lambda kc, yoff=yoff: ybig[:, yoff + kc, :], bro, yb[yoff:yoff + 8])
                        mi = br % 2
                        P.op("act", lambda e, bgt=bgt, mi=mi: e.activation(out=m_e[mi][:, :], in_=ps[bgt][:, :], func=AF.Sigmoid),
                             reads=[psb[bgt]], writes=[m_eb[mi]])
                        if br == 0:
                            P.op("dve", lambda e, bro=bro, mi=mi: e.tensor_tensor(out=m_acc[:, :], in0=ps[bro][:, :], in1=m_e[mi][:, :], op=ALU.mult),
                                 reads=[psb[bro], m_eb[mi]], writes=[m_accb])
                        else:
                            P.op("dve", lambda e, bro=bro, mi=mi: e.tensor_tensor(out=m_e[mi][:, :], in0=ps[bro][:, :], in1=m_e[mi][:, :], op=ALU.mult),
                                 reads=[psb[bro], m_eb[mi]], writes=[m_eb[mi]])
                            if br == 1:
                                P.op("dve", lambda e, mi=mi: e.tensor_tensor(out=m_acc[:, :], in0=m_acc[:, :], in1=m_e[mi][:, :], op=ALU.add),
                                     reads=[m_accb, m_eb[mi]], writes=[m_accb])
                            else:
                                P.op("dve", lambda e, mi=mi, j=j: e.tensor_tensor(out=merged[:, j, :], in0=m_acc[:, :], in1=m_e[mi][:, :], op=ALU.add),
                                     reads=[m_accb, m_eb[mi]], writes=[r2b[j]])
                for j in range(NCH):
                    bo = nbank()
                    proj_group(ws, "w_o", j * 128, NCH, lambda kc: merged[:, kc, :], bo, r2b[0:16])
                    P.op("dve", lambda e, bo=bo, j=j: e.tensor_tensor(out=h[:, j, :], in0=h[:, j, :], in1=ps[bo][:, :], op=ALU.add),
                         reads=[psb[bo]], writes=[hb[j]])
                if dbg and sq == 0 and ti == dbg_tile and l == dbg_layer:
                    dump("d_hmix", h[:, :, :], hb, [128, NCH, T])
                rmsnorm(pr + 16, xn, lambda c: [xb[c]])
                for half in range(2):
                    pre = {}
                    if half == 0:
                        mb = proj_multi(ws, [("w_ff1", c * 128) for c in range(4)])
                        pre = dict(zip(range(4), mb))
                    for c in range(32):
                        if c in pre:
                            bf1 = pre[c]
                        else:
                            bf1 = nbank()
                            proj_group(ws, "w_ff1", (half * 32 + c) * 128, NCH, lambda kc: xn[:, kc, :], bf1, xb)
                        fi = c % 2
                        P.op("act", lambda e, bf1=bf1, fi=fi: e.activation(out=f_r[fi][:, :], in_=ps[bf1][:, :], func=AF.Relu),
                             reads=[psb[bf1]], writes=[f_rb[fi]])
                        P.op("dve", lambda e, fi=fi, c=c: e.tensor_tensor(out=abuf[:, c, :], in0=f_r[fi][:, :], in1=f_r[fi][:, :], op=ALU.mult),
                             reads=[f_rb[fi]], writes=[r2b[c]])
                    for j in range(NCH):
                        b2 = nbank()
                        proj_group(ws, "w_ff2_%d" % half, j * 128, 32, lambda kc: abuf[:, kc, :], b2, r2b)
                        P.op("dve", lambda e, b2=b2, j=j: e.tensor_tensor(out=h[:, j, :], in0=h[:, j, :], in1=ps[b2][:, :], op=ALU.add),
                             reads=[psb[b2]], writes=[hb[j]])
                rmsnorm(pr + 32, xn, lambda c: [xb[c]])
                P.op("pool", lambda e, l=l, sq=sq, t0=t0: e.dma_start(out=pbf[:, :, :], in_=pT[l, sq, :, :, t0:t0 + T]),
                     writes=[pbfb], sem="ldpt")
                pre_g = proj_multi(ws, [("w_pg", j * 128) for j in range(4)])
                for j in range(NCH):
                    if j < 4:
                        bgt = pre_g[j]
                    else:
                        bgt = nbank()
                        proj_group(ws, "w_pg", j * 128, NCH, lambda kc: xn[:, kc, :], bgt, xb)
                    bpp = nbank()
                    proj_group(ws, "w_pi", j * 128, 2, lambda kc: pbf[:, kc, :], bpp, [pbfb])
                    mi = j % 2
                    P.op("act", lambda e, bgt=bgt, mi=mi: e.activation(out=m_e[mi][:, :], in_=ps[bgt][:, :], func=AF.Sigmoid),
                         reads=[psb[bgt]], writes=[m_eb[mi]])
                    P.op("dve", lambda e, bpp=bpp, mi=mi: e.tensor_tensor(out=m_e[mi][:, :], in0=ps[bpp][:, :], in1=m_e[mi][:, :], op=ALU.mult),
                         reads=[psb[bpp], m_eb[mi]], writes=[m_eb[mi]])
                    P.op("dve", lambda e, mi=mi, j=j: e.tensor_tensor(out=h[:, j, :], in0=h[:, j, :], in1=m_e[mi][:, :], op=ALU.add),
                         reads=[m_eb[mi]], writes=[hb[j]])
                if l not in wtiles:
                    wtiles[l] = ws.tiles
                else:
                    assert len(wtiles[l]) == len(ws.tiles) and wtiles[l][-1] == ws.tiles[-1]
                if dbg and sq == 0 and ti == dbg_tile and l == dbg_layer:
                    dump("d_hout", h[:, :, :], hb, [128, NCH, T])
            rmsnorm(2 * PRM_L, ostage, lambda c: [r2b[2 * c], r2b[2 * c + 1]])
            P.op(DMAQ, lambda e, sq=sq, t0=t0: e.dma_start(out=outT[sq, :, :, t0:t0 + T], in_=ostage),
                 reads=r2b, writes=[outb], sem="sto")
    P.wait_all(DMAQ, [outb])
    P.wait_all("pool", dbg_outs)
    P.emit(nc)
    return nc, wtiles


dbg_tile = 0
dbg_layer = 0
DMAQ = "act"
N_ATT = 8
N_HG = 8
HG_NO_INTER = False
HG_NO_CHAIN = False


def _mat_of(inputs, l):
    w_ff2 = inputs["w_ff2"][l]
    m = {"w_in": inputs["w_in"][l], "w_o": inputs["w_o"][l], "w_ff1": inputs["w_ff1"][l],
         "w_ff2_0": w_ff2[0:4096], "w_ff2_1": w_ff2[4096:8192],
         "w_pg": inputs["w_ple_gate"][l], "w_pi": inputs["w_ple_in"][l]}
    for b in range(3):
        m["w_br%d" % b] = inputs["w_branch"][l, b]
    return m


def pack_weights(inputs, wtiles, n_layers=2):
    out = np.zeros((L, NL, 128, LW), np.float32)
    for l in range(n_layers):
        mats = _mat_of(inputs, l)
        flat = out[l].transpose(1, 0, 2).reshape(128, NL * LW)
        flat = np.zeros((128, NL * LW), np.float32)
        for (mat, kc, col, w, ld, off) in wtiles[l]:
            p = ld * LW + off
            flat[:, p:p + w] = mats[mat][kc * 128:(kc + 1) * 128, col:col + w]
        out[l] = flat.reshape(128, NL, LW).transpose(1, 0, 2)
    return out


def build_consts():
    c = np.zeros((128, NCST), np.float32)
    c[:, 0:128] = np.eye(128, dtype=np.float32)
    c[:, 128:256] = 1.0
    s = np.arange(128)[:, None]
    t = np.arange(128)[None, :]
    c[:, 256:384] = ((s // 64 == t // 64) & (s <= t)).astype(np.float32)
    r = np.ones(512, np.float32)
    r[::64] = 0.0
    c[:, 384:896] = r[None, :]
    return c


def build_params(inputs):
    p = np.zeros((128, NPRM), np.float32)
    for l in range(L):
        o = l * PRM_L
        p[:, o:o + 16] = inputs["g_mix"][l].reshape(16, 128).T
        p[:, o + 16:o + 32] = inputs["g_ff"][l].reshape(16, 128).T
        p[:, o + 32:o + 48] = inputs["g_ple"][l].reshape(16, 128).T
        cw = inputs["conv_w"][l]
        p[:, o + 48:o + 72] = cw.reshape(3, 8, 128).transpose(2, 1, 0).reshape(128, 24)
        p[:, o + 72:o + 80] = inputs["hg_lb_logits"][l].reshape(8, 128).T
        p[:, o + 80] = inputs["hg_norm_g"][l]
    p[:, 2 * PRM_L:2 * PRM_L + 16] = inputs["g_final"].reshape(16, 128).T
    return p


def build_bias_tables(inputs):
    tab = inputs["att_rel_bias"]
    k = np.arange(128)
    c2 = (k // 64)[:, None, None]
    b = (k % 64)[:, None, None]
    e = np.arange(10)[None, :, None]
    a = np.arange(64)[None, None, :]
    d = e - c2
    rel = d * 64 + a - b
    idx = np.clip(rel, -256, 256) + 256
    valid = (d >= 0) & (d <= 8)
    idx = np.broadcast_to(idx, (128, 10, 64))
    valid = np.broadcast_to(valid, (128, 10, 64))
    g = tab[:, :, idx]
    g = np.where(valid[None, None], g, np.float32(NEG)).astype(np.float32)
    return np.ascontiguousarray(g.reshape(L, 8, 128, 640))


_CACHE = {}


def _get_program(n_seq, n_tiles, n_layers, dbg):
    key = (n_seq, n_tiles, n_layers, dbg)
    if key not in _CACHE:
        _CACHE[key] = build_program(n_seq, n_tiles, n_layers, dbg)
    return _CACHE[key]


def kernel(x, p, w_in, conv_w, hg_lb_logits, hg_norm_g, att_rel_bias, w_branch, w_o,
           w_ff1, w_ff2, w_ple_in, w_ple_gate, g_mix, g_ff, g_ple, g_final):
    inputs = dict(x=x, p=p, w_in=w_in, conv_w=conv_w, hg_lb_logits=hg_lb_logits, hg_norm_g=hg_norm_g,
                  att_rel_bias=att_rel_bias, w_branch=w_branch, w_o=w_o, w_ff1=w_ff1, w_ff2=w_ff2,
                  w_ple_in=w_ple_in, w_ple_gate=w_ple_gate, g_mix=g_mix, g_ff=g_ff, g_ple=g_ple, g_final=g_final)
    inputs = {k: np.asarray(v, dtype=np.float32) for k, v in inputs.items()}
    B, S, _ = inputs["x"].shape
    n_cores = 8
    n_seq = B // n_cores
    n_tiles = S // T
    nc, wtiles = _get_program(n_seq, n_tiles, L, False)
    wpk = pack_weights(inputs, wtiles)
    prm = build_params(inputs)
    cst = build_consts()
    btd = build_bias_tables(inputs)
    in_maps = []
    for c in range(n_cores):
        xs = inputs["x"][c * n_seq:(c + 1) * n_seq]
        xT = np.ascontiguousarray(xs.reshape(n_seq, S, NCH, 128).transpose(0, 3, 2, 1))
        ps_ = inputs["p"][:, c * n_seq:(c + 1) * n_seq]
        pT = np.ascontiguousarray(ps_.reshape(L, n_seq, S, 2, 128).transpose(0, 1, 4, 3, 2))
        in_maps.append({"xT": xT, "pT": pT, "wpk": wpk, "prm": prm, "cst": cst, "btd": btd})
    res = run_bass_kernel_spmd(nc, in_maps, core_ids=list(range(n_cores)))
    out = np.empty((B, S, D), np.float32)
    for c in range(n_cores):
        oT = res.results[c]["outT"]
        out[c * n_seq:(c + 1) * n_seq] = oT.transpose(0, 3, 2, 1).reshape(n_seq, S, D)
    return out
```
